# Optimizing a Trainium2 kernel written in Bass

```python
import math
import jax, jax.numpy as jnp
from jax import lax
import numpy as np

D_MODEL = 2048
BATCH = 2
SEQ = 8192
DEPTH = 1

CHUNK = 64
D_MIX = D_MODEL
RET_WIDTH = D_MIX // 2
RET_HEAD_DIM = 256
RET_HEADS = RET_WIDTH // RET_HEAD_DIM
SSD_WIDTH = D_MIX - RET_WIDTH
SSD_HEAD_DIM = 64
SSD_HEADS = SSD_WIDTH // SSD_HEAD_DIM
SSD_GROUPS = 2
SSD_HEADS_PER_GROUP = SSD_HEADS // SSD_GROUPS
SSD_STATE = 128
CONV_WIDTH = 4
CONV_CH = SSD_WIDTH + 2 * SSD_GROUPS * SSD_STATE
D_IN_PROJ = 4 * RET_WIDTH + SSD_WIDTH + CONV_CH + SSD_HEADS
D_FF = ((8 * D_MODEL // 3 + 255) // 256) * 256
ROPE_BASE = 10000.0
LN_EPS = 1e-5
DEEPNORM_ALPHA = (2.0 * DEPTH) ** 0.25
DEEPNORM_BETA = (8.0 * DEPTH) ** -0.25
FFN_RES_WEIGHT = 0.5

kernel_name = "hymba_retention_ssd_macaron_deepnorm"


def layer_norm(x, gain, bias):
    xf = x.astype(jnp.float32)
    mu = jnp.mean(xf, axis=-1, keepdims=True)
    var = jnp.mean(jnp.square(xf - mu), axis=-1, keepdims=True)
    return ((xf - mu) * lax.rsqrt(var + LN_EPS)).astype(x.dtype) * gain + bias


def swiglu_ffn(x, w_gate, w_up, w_down):
    return (jax.nn.silu(x @ w_gate) * (x @ w_up)) @ w_down


def rotate_every_two(t, positions):
    half = t.shape[-1] // 2
    inv_freq = 1.0 / (ROPE_BASE ** jnp.linspace(0.0, 1.0, half, dtype=jnp.float32))
    theta = positions.astype(jnp.float32)[..., None] * inv_freq
    cos = jnp.cos(theta)[:, :, None, :].astype(t.dtype)
    sin = jnp.sin(theta)[:, :, None, :].astype(t.dtype)
    t2 = t.reshape(t.shape[:-1] + (half, 2))
    t_even, t_odd = t2[..., 0], t2[..., 1]
    out = jnp.stack([t_even * cos - t_odd * sin, t_odd * cos + t_even * sin], axis=-1)
    return out.reshape(t.shape)


def multiscale_retention(q, k, v, positions):
    b, l, _ = q.shape
    nc = l // CHUNK
    dtype = q.dtype
    shp = (b, l, RET_HEADS, RET_HEAD_DIM)
    q = rotate_every_two(q.reshape(shp), positions)
    k = rotate_every_two(k.reshape(shp), positions) * (RET_HEAD_DIM ** -0.5)
    v = v.reshape(shp)
    log_gamma = jnp.log(1.0 - 2.0 ** (-5.0 - jnp.arange(RET_HEADS, dtype=jnp.float32)))
    pos = jnp.arange(CHUNK, dtype=jnp.float32)
    rel = pos[:, None] - pos[None, :]
    intra_decay = jnp.where(rel >= 0, jnp.exp(log_gamma[:, None, None] * jnp.maximum(rel, 0.0)), 0.0).astype(dtype)
    query_decay = jnp.exp(log_gamma[None, :] * (pos[:, None] + 1.0)).astype(dtype)
    key_decay = jnp.exp(log_gamma[None, :] * (CHUNK - 1.0 - pos[:, None])).astype(dtype)
    chunk_decay = jnp.exp(log_gamma * CHUNK).astype(dtype)
    cshp = (b, nc, CHUNK, RET_HEADS, RET_HEAD_DIM)
    qc, kc, vc = q.reshape(cshp), k.reshape(cshp), v.reshape(cshp)
    scores = jnp.einsum('bzihd,bzjhd->bzhij', qc, kc) * intra_decay
    o_intra = jnp.einsum('bzhij,bzjhe->bzihe', scores, vc)

    def step(state, inp):
        q_z, k_z, v_z = inp
        o_z = jnp.einsum('bihd,bhde->bihe', q_z, state) * query_decay[None, :, :, None]
        state = state * chunk_decay[None, :, None, None] + jnp.einsum('bjhd,bjhe,jh->bhde', k_z, v_z, key_decay)
        return state, o_z

    state0 = jnp.zeros((b, RET_HEADS, RET_HEAD_DIM, RET_HEAD_DIM), dtype)
    _, o_inter = lax.scan(step, state0, (jnp.moveaxis(qc, 1, 0), jnp.moveaxis(kc, 1, 0), jnp.moveaxis(vc, 1, 0)))
    o = o_intra + jnp.moveaxis(o_inter, 0, 1)
    return o.reshape(shp)


def head_group_norm(o, gain, bias):
    of = o.astype(jnp.float32)
    mu = jnp.mean(of, axis=-1, keepdims=True)
    var = jnp.mean(jnp.square(of - mu), axis=-1, keepdims=True)
    on = ((of - mu) * lax.rsqrt(var + LN_EPS)).astype(o.dtype)
    b, l = o.shape[0], o.shape[1]
    return on.reshape(b, l, RET_WIDTH) * gain + bias


def ssd_mixer(z, xbc, dt_raw, conv_w, conv_b, dt_bias, a_log, d_skip, norm_gain):
    b, l, _ = z.shape
    nc = l // CHUNK
    G, R, P, N = SSD_GROUPS, SSD_HEADS_PER_GROUP, SSD_HEAD_DIM, SSD_STATE
    xpad = jnp.pad(xbc, ((0, 0), (CONV_WIDTH - 1, 0), (0, 0)))
    conv = conv_b + sum(xpad[:, t:t + l] * conv_w[t] for t in range(CONV_WIDTH))
    xbc = jax.nn.silu(conv)
    xs, bm, cm = jnp.split(xbc, [SSD_WIDTH, SSD_WIDTH + G * N], axis=-1)
    x = xs.reshape(b, nc, CHUNK, G, R, P)
    bm = bm.reshape(b, nc, CHUNK, G, N)
    cm = cm.reshape(b, nc, CHUNK, G, N)
    dt = jax.nn.softplus((dt_raw + dt_bias).astype(jnp.float32)).reshape(b, nc, CHUNK, G, R)
    a = -jnp.exp(a_log.astype(jnp.float32)).reshape(G, R)
    a_cum = jnp.cumsum(dt * a, axis=2)
    xdt = x * dt[..., None].astype(x.dtype)
    seg = a_cum[:, :, :, None] - a_cum[:, :, None, :]
    idx = jnp.arange(CHUNK)
    causal = (idx[:, None] >= idx[None, :])[None, None, :, :, None, None]
    decay = jnp.exp(jnp.where(causal, seg, -jnp.inf)).astype(x.dtype)
    cb = jnp.einsum('bzign,bzjgn->bzijg', cm, bm)
    y_intra = jnp.einsum('bzijgr,bzjgrp->bzigrp', cb[..., None] * decay, xdt)

    def step(state, inp):
        c_z, b_z, xdt_z, acum_z = inp
        from_start = jnp.exp(acum_z).astype(state.dtype)
        to_end = jnp.exp(acum_z[:, -1:] - acum_z).astype(state.dtype)
        total = jnp.exp(acum_z[:, -1]).astype(state.dtype)
        y_z = jnp.einsum('bign,bgrpn->bigrp', c_z, state) * from_start[..., None]
        state = state * total[..., None, None] + jnp.einsum('bjgn,bjgr,bjgrp->bgrpn', b_z, to_end, xdt_z)
        return state, y_z

    state0 = jnp.zeros((b, G, R, P, N), x.dtype)
    _, y_inter = lax.scan(step, state0, (jnp.moveaxis(cm, 1, 0), jnp.moveaxis(bm, 1, 0),
                                         jnp.moveaxis(xdt, 1, 0), jnp.moveaxis(a_cum, 1, 0)))
    y = y_intra + jnp.moveaxis(y_inter, 0, 1) + x * d_skip.reshape(G, R)[:, :, None]
    y = y.reshape(b, l, SSD_WIDTH) * jax.nn.silu(z)
    yg = y.reshape(b, l, G, SSD_WIDTH // G).astype(jnp.float32)
    yg = yg * lax.rsqrt(jnp.mean(jnp.square(yg), axis=-1, keepdims=True) + LN_EPS)
    return yg.reshape(b, l, SSD_WIDTH).astype(z.dtype) * norm_gain


def hybrid_mixer(h, positions, w_in, ret_gn_gain, ret_gn_bias, conv_w, conv_b,
                 dt_bias, a_log, d_skip, ssd_norm_gain, w_out):
    proj = h @ w_in
    splits = [RET_WIDTH, 2 * RET_WIDTH, 3 * RET_WIDTH, 4 * RET_WIDTH,
              4 * RET_WIDTH + SSD_WIDTH, 4 * RET_WIDTH + SSD_WIDTH + CONV_CH]
    q, k, v, g, z, xbc, dt_raw = jnp.split(proj, splits, axis=-1)
    ret = multiscale_retention(q, k, v, positions)
    ret_out = jax.nn.silu(g) * head_group_norm(ret, ret_gn_gain, ret_gn_bias)
    ssd_out = ssd_mixer(z, xbc, dt_raw, conv_w, conv_b, dt_bias, a_log, d_skip, ssd_norm_gain)
    return jnp.concatenate([ret_out, ssd_out], axis=-1) @ w_out


def setup_inputs(seed: int = 0) -> dict:
    key = jax.random.key(seed)
    ks = jax.random.split(key, 26)
    f32 = jnp.float32

    def normal(k, shape, scale):
        return jax.random.normal(k, shape, f32) * scale

    x = jax.random.normal(ks[0], (BATCH, SEQ, D_MODEL), f32)
    offset = jax.random.randint(ks[1], (BATCH, 1), 0, 1024, dtype=jnp.int32) * CHUNK
    positions = (offset + jnp.arange(SEQ, dtype=jnp.int32)[None, :]).astype(jnp.int32)
    dt_init = jnp.exp(jax.random.uniform(ks[13], (DEPTH, SSD_HEADS), f32, math.log(1e-3), math.log(1e-1)))
    ssd_dt_bias = dt_init + jnp.log(-jnp.expm1(-dt_init))
    ssd_a_log = jnp.log(jax.random.uniform(ks[14], (DEPTH, SSD_HEADS), f32, 1.0, 16.0))
    return {
        "x": x,
        "positions": positions,
        "ffn1_w_gate": normal(ks[2], (DEPTH, D_MODEL, D_FF), D_MODEL ** -0.5),
        "ffn1_w_up": normal(ks[3], (DEPTH, D_MODEL, D_FF), D_MODEL ** -0.5),
        "ffn1_w_down": normal(ks[4], (DEPTH, D_FF, D_MODEL), DEEPNORM_BETA * D_FF ** -0.5),
        "ln1_gain": 1.0 + normal(ks[5], (DEPTH, D_MODEL), 0.02),
        "ln1_bias": normal(ks[6], (DEPTH, D_MODEL), 0.02),
        "mix_w_in": normal(ks[7], (DEPTH, D_MODEL, D_IN_PROJ), D_MODEL ** -0.5),
        "ret_gn_gain": 1.0 + normal(ks[8], (DEPTH, RET_WIDTH), 0.02),
        "ret_gn_bias": normal(ks[9], (DEPTH, RET_WIDTH), 0.02),
        "ssd_conv_w": normal(ks[10], (DEPTH, CONV_WIDTH, CONV_CH), CONV_WIDTH ** -0.5),
        "ssd_conv_b": normal(ks[11], (DEPTH, CONV_CH), 0.02),
        "ssd_dt_bias": ssd_dt_bias,
        "ssd_a_log": ssd_a_log,
        "ssd_d": 1.0 + normal(ks[15], (DEPTH, SSD_HEADS), 0.02),
        "ssd_norm_gain": 1.0 + normal(ks[16], (DEPTH, SSD_WIDTH), 0.02),
        "mix_w_out": normal(ks[17], (DEPTH, D_MIX, D_MODEL), DEEPNORM_BETA * D_MIX ** -0.5),
        "ln2_gain": 1.0 + normal(ks[18], (DEPTH, D_MODEL), 0.02),
        "ln2_bias": normal(ks[19], (DEPTH, D_MODEL), 0.02),
        "ffn2_w_gate": normal(ks[20], (DEPTH, D_MODEL, D_FF), D_MODEL ** -0.5),
        "ffn2_w_up": normal(ks[21], (DEPTH, D_MODEL, D_FF), D_MODEL ** -0.5),
        "ffn2_w_down": normal(ks[22], (DEPTH, D_FF, D_MODEL), DEEPNORM_BETA * D_FF ** -0.5),
        "ln3_gain": 1.0 + normal(ks[23], (DEPTH, D_MODEL), 0.02),
        "ln3_bias": normal(ks[24], (DEPTH, D_MODEL), 0.02),
    }


def reference(x, positions, ffn1_w_gate, ffn1_w_up, ffn1_w_down, ln1_gain, ln1_bias,
              mix_w_in, ret_gn_gain, ret_gn_bias, ssd_conv_w, ssd_conv_b, ssd_dt_bias,
              ssd_a_log, ssd_d, ssd_norm_gain, mix_w_out, ln2_gain, ln2_bias,
              ffn2_w_gate, ffn2_w_up, ffn2_w_down, ln3_gain, ln3_bias):
    for layer in range(DEPTH):
        x = layer_norm(DEEPNORM_ALPHA * x + FFN_RES_WEIGHT * swiglu_ffn(x, ffn1_w_gate[layer], ffn1_w_up[layer], ffn1_w_down[layer]),
                       ln1_gain[layer], ln1_bias[layer])
        mix = hybrid_mixer(x, positions, mix_w_in[layer], ret_gn_gain[layer], ret_gn_bias[layer],
                           ssd_conv_w[layer], ssd_conv_b[layer], ssd_dt_bias[layer], ssd_a_log[layer],
                           ssd_d[layer], ssd_norm_gain[layer], mix_w_out[layer])
        x = layer_norm(DEEPNORM_ALPHA * x + mix, ln2_gain[layer], ln2_bias[layer])
        x = layer_norm(DEEPNORM_ALPHA * x + FFN_RES_WEIGHT * swiglu_ffn(x, ffn2_w_gate[layer], ffn2_w_up[layer], ffn2_w_down[layer]),
                       ln3_gain[layer], ln3_bias[layer])
    return x
```

```python
import math
import os
DBG = int(os.environ.get('KDBG', '0'))
import contextlib
import numpy as np
import concourse.bass as bass
import concourse.mybir as mybir
from concourse.bass_utils import run_bass_kernel_spmd

F32 = mybir.dt.float32
BF16 = mybir.dt.bfloat16
I32 = mybir.dt.int32
AF = mybir.ActivationFunctionType
ALU = mybir.AluOpType

D = 2048
DFF = 5632
NFC = 44
TOK = 2048
TT = 512
NT = TOK // TT
CH = 128
DIN = 6672
ALPHA = 2.0 ** 0.25
EPS = 1e-5
GAM = [1.0 - 2.0 ** (-5.0 - h) for h in range(4)]

SB_LO = 16384 + 1024
SB_HI = 229376 - 256


class V:
    __slots__ = ("ap", "space", "lo", "hi")

    def __init__(self, ap, space, lo, hi):
        self.ap, self.space, self.lo, self.hi = ap, space, lo, hi


class Buf:
    def __init__(self, h, space, base, shape, esz, pdim=True):
        self.h, self.space, self.base, self.shape, self.esz, self.pdim = h, space, base, list(shape), esz, pdim
        st = [1] * len(shape)
        for i in range(len(shape) - 2, -1, -1):
            st[i] = st[i + 1] * shape[i + 1]
        self.st = st
        self.nbytes = (st[1] * shape[1] if pdim else st[0] * shape[0]) * esz

    def __getitem__(self, idx):
        if not isinstance(idx, tuple):
            idx = (idx,)
        idx = tuple(idx) + (slice(None),) * (len(self.shape) - len(idx))
        lo = 0
        hi = 0
        for d, (i, n, s) in enumerate(zip(idx, self.shape, self.st)):
            if self.pdim and d == 0:
                continue
            if isinstance(i, slice):
                a = 0 if i.start is None else i.start
                b = n if i.stop is None else i.stop
            else:
                a, b = i, i + 1
            lo += a * s
            hi += (b - 1) * s
        return V(self.h[idx], self.space, self.base + lo * self.esz, self.base + (hi + 1) * self.esz)

    def whole(self):
        return self[tuple(slice(None) for _ in self.shape)]

    def cust(self, ap, lo_el, hi_el):
        return V(ap, self.space, self.base + lo_el * self.esz, self.base + hi_el * self.esz)


class Sched:
    ENG = ("pe", "act", "dve", "pool", "sp")

    def __init__(self):
        self.ops = {e: [] for e in self.ENG}
        self.Wr = {}
        self.Rd = {}
        self.slot_count = {}
        self.slot_batch = {}

    @staticmethod
    def _rng(v):
        if v.space == "ps":
            b = v.lo // 2048
            return b * 2048, (b + 1) * 2048
        return v.lo, v.hi

    def op(self, eng, fn, reads=(), writes=(), slot=None, batch=False, cc=False):
        deps = {}

        def add(ref):
            k = ref[:2]
            if deps.get(k, -1) < ref[2]:
                deps[k] = ref[2]

        for v in reads:
            vlo, vhi = self._rng(v)
            for (lo, hi, ref) in self.Wr.get(v.space, ()):
                if lo < vhi and vlo < hi:
                    add(ref)
            if v.space == "ps":
                for (lo, hi, k), ref in self.Rd.get(v.space, {}).items():
                    if lo < vhi and vlo < hi and k != ("c", eng):
                        add(ref)
        for v in writes:
            vlo, vhi = self._rng(v)
            for (lo, hi, ref) in self.Wr.get(v.space, ()):
                if lo < vhi and vlo < hi:
                    add(ref)
            for (lo, hi, _k), ref in self.Rd.get(v.space, {}).items():
                if lo < vhi and vlo < hi:
                    add(ref)
        idx = len(self.ops[eng])
        if slot is not None:
            c = self.slot_count.get(slot, 0) + 1
            self.slot_count[slot] = c
            self.slot_batch[slot] = (batch, cc)
            ref = ("d", slot, c)
        else:
            ref = ("c", eng, idx)
        self.ops[eng].append(dict(fn=fn, deps=deps, slot=slot, ref=ref, needed=False))
        for v in reads:
            vlo, vhi = self._rng(v)
            self.Rd.setdefault(v.space, {})[(vlo, vhi, ref[:2])] = ref
        for v in writes:
            vlo, vhi = self._rng(v)
            rd = self.Rd.get(v.space)
            if rd:
                for k in [k for k in rd if vlo <= k[0] and k[1] <= vhi]:
                    del rd[k]
            wl = self.Wr.setdefault(v.space, [])
            wl[:] = [w for w in wl if not (vlo <= w[0] and w[1] <= vhi)]
            wl.append((vlo, vhi, ref))

    def check_deadlock(self, same_eng_sync=True):
        sem = {}
        pc = {e: 0 for e in self.ENG}
        progress = True
        while progress:
            progress = False
            for eng in self.ENG:
                ops = self.ops[eng]
                while pc[eng] < len(ops):
                    o = ops[pc[eng]]
                    ok = True
                    for (kind, name), val in o["deps"].items():
                        if kind == "c":
                            if name == eng and (eng == "pe" or not same_eng_sync):
                                continue
                            key, target = ("c", name), self.ops[name][val]["val"]
                        else:
                            batch, cc = self.slot_batch[name]
                            n = self.slot_count[name] if batch else val
                            key, target = ("d", name), n
                        if sem.get(key, 0) < target:
                            ok = False
                            break
                    if not ok:
                        break
                    if o["slot"] is not None:
                        sem[("d", o["slot"])] = sem.get(("d", o["slot"]), 0) + 1
                    elif o["needed"]:
                        sem[("c", eng)] = sem.get(("c", eng), 0) + 1
                    pc[eng] += 1
                    progress = True
        stuck = {e: (pc[e], len(self.ops[e])) for e in self.ENG if pc[e] < len(self.ops[e])}
        if stuck:
            for e, (p, n) in stuck.items():
                print("DEADLOCK", e, p, n, self.ops[e][p]["deps"], self.ops[e][p]["ref"])
            raise RuntimeError("semaphore program deadlocks: %s" % stuck)
        print("deadlock check ok:", {e: len(self.ops[e]) for e in self.ENG}, "max sem", max(sem.values()))

    def emit(self, nc, same_eng_sync=True):
        for eng, ops in self.ops.items():
            for o in ops:
                for (kind, name), val in o["deps"].items():
                    if kind == "c":
                        if name == eng and (eng == "pe" or not same_eng_sync):
                            continue
                        self.ops[name][val]["needed"] = True
        for eng, ops in self.ops.items():
            c = 0
            for o in ops:
                if o["needed"]:
                    c += 1
                o["val"] = c
        self.check_deadlock(same_eng_sync)
        with contextlib.ExitStack() as es:
            esem = {e: es.enter_context(nc.semaphore("se_" + e)) for e in ("pe", "act", "dve", "pool")}
            ssem = {s: es.enter_context(nc.semaphore("sd_%d" % i)) for i, s in enumerate(self.slot_count)}
            block = es.enter_context(nc.Block())
            sched = self

            def run(eng, e):
                seen = {}
                for o in sched.ops[eng]:
                    for (kind, name), val in o["deps"].items():
                        if kind == "c":
                            if name == eng and (eng == "pe" or not same_eng_sync):
                                continue
                            sem, target = esem[name], sched.ops[name][val]["val"]
                        else:
                            batch, cc = sched.slot_batch[name]
                            n = sched.slot_count[name] if batch else val
                            sem, target = ssem[name], (n if cc else 16 * n)
                        if seen.get((kind, name), 0) >= target:
                            continue
                        seen[(kind, name)] = target
                        e.wait_ge(sem, target)
                    ins = o["fn"](e)
                    if o["slot"] is not None:
                        if sched.slot_batch[o["slot"]][1]:
                            ins.then_inc(ssem[o["slot"]])
                        else:
                            ins.then_inc(ssem[o["slot"]], 16)
                    elif o["needed"]:
                        ins.then_inc(esem[eng], 1)
                for s, n in sched.slot_count.items():
                    if sched.slot_owner.get(s) == eng:
                        cc = sched.slot_batch[s][1]
                        e.wait_ge(ssem[s], n if cc else 16 * n)

            self.slot_owner = {}
            for eng, ops in self.ops.items():
                for o in ops:
                    if o["slot"] is not None:
                        self.slot_owner[o["slot"]] = eng

            @block.tensor
            def _(e):
                run("pe", e)

            @block.scalar
            def _(e):
                run("act", e)

            @block.vector
            def _(e):
                run("dve", e)

            @block.gpsimd
            def _(e):
                run("pool", e)

            @block.sync
            def _(e):
                run("sp", e)


_off = {}
_n = 0
for _name, _w in (("ident", 128), ("U", 128), ("ones", 128), ("cmask", 128), ("dmask", 512), ("qdec", 4),
                  ("kdec", 4), ("invf", 1), ("convw", 48), ("convb", 12), ("gng", 8), ("gnb", 8), ("ng", 8),
                  ("dcol", 8), ("dtb", 16), ("alog", 16), ("ohalo", 4), ("mr", 7), ("retw", 28)):
    _off[_name] = (_n, _n + _w)
    _n += _w
NCST = _n


def _host_consts(core, ssd_conv_w, ssd_conv_b, ret_gn_gain, ret_gn_bias, ssd_norm_gain, ssd_d, ssd_dt_bias, ssd_a_log):
    c = np.zeros((128, NCST), np.float32)

    def put(name, arr):
        a, b = _off[name]
        c[:, a:b] = np.asarray(arr, np.float32).reshape(128, b - a)

    p = np.arange(128)
    put("ident", np.eye(128))
    put("U", (p[:, None] <= p[None, :]).astype(np.float32))
    put("ones", np.ones((128, 128)))
    put("cmask", np.where(p[None, :] >= p[:, None], 0.0, -30000.0))
    dm = np.zeros((128, 4, 128), np.float64)
    for h in range(4):
        rel = (p[None, :] - p[:, None]).astype(np.float64)
        dm[:, h, :] = np.where(rel >= 0, GAM[h] ** np.maximum(rel, 0.0), 0.0) / 16.0
    put("dmask", dm)
    put("qdec", np.stack([GAM[h] ** (p + 1.0) for h in range(4)], axis=1))
    put("kdec", np.stack([GAM[h] ** (127.0 - p) / 16.0 for h in range(4)], axis=1))
    invf = (1.0 / (np.float32(10000.0) ** np.linspace(0.0, 1.0, 128, dtype=np.float32))).astype(np.float32)
    put("invf", invf)
    put("convw", ssd_conv_w.reshape(4, 12, 128).transpose(2, 1, 0))
    put("convb", ssd_conv_b.reshape(12, 128).T)
    put("gng", ret_gn_gain.reshape(8, 128).T)
    put("gnb", ret_gn_bias.reshape(8, 128).T)
    put("ng", ssd_norm_gain.reshape(8, 128).T)
    put("dcol", np.repeat(ssd_d.reshape(16), 64).reshape(8, 128).T)
    put("dtb", np.broadcast_to(ssd_dt_bias.reshape(1, 16), (128, 16)))
    put("alog", np.broadcast_to(ssd_a_log.reshape(1, 16), (128, 16)))
    rank, seq = core % 4, core // 4
    oh = np.zeros(4)
    if rank > 0:
        oh[rank - 1] = 1.0
    put("ohalo", np.broadcast_to(oh, (128, 4)))
    mr = np.array([1.0 if (r8 // 4 == seq and r8 % 4 < rank) else 0.0 for r8 in range(7)])
    put("mr", np.broadcast_to(mr, (128, 7)))
    rw = np.array([[(GAM[h] ** 2048.0) if mr[r8] > 0 else 1.0 for h in range(4)] for r8 in range(7)]).reshape(28)
    put("retw", np.broadcast_to(rw, (128, 28)))
    return c


GU_W = 22 * 8192
D_W = 16 * 5632
IO_IN = 13 * 8192
IO_DT = IO_IN
IO_OUT = IO_IN + 256
IO_W = 69 * 2048
NST = 2048 + 1024 + 16
TWO_PI = 2.0 * math.pi
CW1 = 6.28125
CW2 = TWO_PI - CW1


def build(stage=3, tiles_a1=(3, 0, 1, 2)):
    nc = bass.Bass("TRN2", target_bir_lowering=False)
    S = Sched()
    es = contextlib.ExitStack()
    es.enter_context(nc.allow_low_precision("bf16 matmul operands, fp32 accumulation"))

    def dram_in(name, shape, dt):
        t = nc.dram_tensor(name, shape, dt, kind="ExternalInput")
        return Buf(t.ap(), name, 0, shape, 4 if dt != BF16 else 2, pdim=False)

    def dram_tmp(name, shape, dt):
        t = nc.dram_tensor(name, shape, dt)
        return Buf(t.ap(), name, 0, shape, 4 if dt != BF16 else 2, pdim=False)

    x_d = dram_in("x", [TOK, D], F32)
    pos_d = dram_in("pos", [128, TOK], I32)
    cst_d = dram_in("cst", [128, NCST], F32)
    lngb_d = dram_in("lngb", [6, 128, D], F32)
    wg_d = [dram_in("wg%d" % i, [16, 16, DFF], F32) for i in (1, 2)]
    wu_d = [dram_in("wu%d" % i, [16, 16, DFF], F32) for i in (1, 2)]
    wd_d = [dram_in("wd%d" % i, [NFC, 16, D], F32) for i in (1, 2)]
    win_d = dram_in("win", [16, 16, DIN], F32)
    wout_d = dram_in("wout", [16, 16, D], F32)
    out_t = nc.dram_tensor("out", [TOK, D], F32, kind="ExternalOutput")
    out_d = Buf(out_t.ap(), "out", 0, [TOK, D], 4, pdim=False)

    def ag_pair(name, width):
        j = width // 2048
        tl = nc.dram_tensor(name + "_l", [16 * j, 2048], BF16)
        tg = nc.dram_tensor(name + "_g", [128 * j, 2048], BF16)
        bl = Buf(tl.ap().rearrange("(pp j) n -> pp (j n)", pp=16), name + "_l", 0, [16, width], 2, pdim=False)
        bg = Buf(tg.ap().rearrange("(pp j) n -> pp (j n)", pp=128), name + "_g", 0, [128, width], 2, pdim=False)
        bl.raw, bg.raw = tl.ap(), tg.ap()
        return bl, bg

    gu_l, gu_g, d_l, d_g = [], [], [], []
    for i in range(2):
        a, b = ag_pair("gu%d" % i, GU_W)
        gu_l.append(a)
        gu_g.append(b)
        a, b = ag_pair("dd%d" % i, D_W)
        d_l.append(a)
        d_g.append(b)
    io_l, io_g = ag_pair("io", IO_W)
    GU_SPLIT = 6
    gu0a_l, gu0a_g = ag_pair("gu0a", GU_SPLIT * 8192)
    gu0b_l, gu0b_g = ag_pair("gu0b", (22 - GU_SPLIT) * 8192)

    def gu_step(i, s_):
        if i == 0:
            if s_ < GU_SPLIT:
                return gu0a_g[:, s_ * 8192:(s_ + 1) * 8192]
            return gu0b_g[:, (s_ - GU_SPLIT) * 8192:(s_ - GU_SPLIT + 1) * 8192]
        return gu_g[i][:, s_ * 8192:(s_ + 1) * 8192]
    x1s = dram_tmp("x1s", [NT, 128, 4, D], F32)
    x1Ts = dram_tmp("x1Ts", [NT, 128, 16, TT], BF16)
    hsend = dram_tmp("hsend", [128, 36], F32)
    hrecv = dram_tmp("hrecv", [4 * 128, 36], F32)
    ssend = dram_tmp("ssend", [128, NST], F32)
    srecv = dram_tmp("srecv", [8 * 128, NST], F32)

    cur = [SB_LO]

    def sb(name, shape, dt, addr=None):
        esz = 2 if dt == BF16 else 4
        n = esz
        for s in shape[1:]:
            n *= s
        if addr is None:
            addr = cur[0]
            cur[0] += (n + 63) // 64 * 64
            assert cur[0] <= SB_HI, (name, cur[0])
        else:
            assert addr + n <= SB_HI, (name, addr, n)
        h = nc.alloc_sbuf_tensor_at(name, shape, dt, offset=addr)
        return Buf(h, "sb", addr, shape, esz)

    cst = sb("cst", [128, NCST], F32)

    def C(name, a=None, b=None):
        lo, hi = _off[name]
        if a is None:
            return cst[:, lo:hi]
        return cst[:, lo + a:lo + (b if b is not None else a + 1)]

    ident_bf = sb("ident_bf", [128, 128], BF16)
    dmat = sb("dmat", [128, 8, 128], BF16)
    a_bc = sb("a_bc", [128, 16], F32)
    S_f = sb("S_f", [128, 2048], F32)
    S_b = sb("S_b", [128, 2048], BF16)
    st_f = sb("st_f", [128, 1024], F32)
    st_b = sb("st_b", [128, 1024], BF16)
    halo = sb("halo", [128, 12, 3], F32)
    halo0 = sb("halo0", [128, 12, 3], F32)
    xl3 = sb("xl3", [128, 16, 3], BF16)
    small = sb("small", [128, 384], F32)
    atot = sb("atot", [128, 16], F32)
    NW = 3
    Wp = [sb("W%d" % i, [128, 8192], BF16) for i in range(NW)]
    xT = [sb("xT%d" % i, [128, 16, TT], BF16) for i in range(2)]
    gb = [[sb("gb%d%d" % (i, k), [128, D], F32, addr=xT[i].base + k * 8192) for k in range(2)] for i in range(2)]
    xtok = sb("xtok", [128, 4, D], F32)
    rbuf = sb("rbuf", [128, NST], F32, addr=xtok.base)
    rbuf2 = sb("rbuf2", [128, NST], F32, addr=xtok.base + 16384)
    hr = sb("hr", [128, 4, 36], F32, addr=xtok.base + 16384)
    arena0 = cur[0]
    ARENA = SB_HI - arena0
    hT = sb("hT", [128, NFC, TT], BF16, addr=arena0)
    sgt = [sb("sgt%d" % i, [128, TT], F32, addr=arena0 + 45056 + i * 2048) for i in range(2)]
    x1T_stage = sb("x1Tst", [128, 16, TT], BF16, addr=arena0 + 28 * 1024)
    lnst = sb("lnst", [128, 24], F32, addr=arena0 + 45056 + 4096)
    lnmv = sb("lnmv", [128, 4, 4], F32, addr=arena0 + 45056 + 4096 + 128)
    assert 45056 + 4096 + 256 <= ARENA, ARENA

    mcur = [arena0]

    def ma(name, shape, dt):
        esz = 2 if dt == BF16 else 4
        n = esz
        for s in shape[1:]:
            n *= s
        addr = mcur[0]
        mcur[0] += (n + 63) // 64 * 64
        assert mcur[0] <= SB_HI, (name, mcur[0] - arena0, ARENA)
        return sb(name, shape, dt, addr=addr)

    mixT = ma("mixT", [128, 16, TT], BF16)
    mbase = mcur[0]
    xbcT = ma("xbcT", [128, 12, TT], BF16)
    sz = ma("sz", [128, 4, 1024], BF16)
    pre = [ma("pre%d" % i, [128, TT + 3], F32) for i in range(2)]
    acc = [ma("acc%d" % i, [128, TT], F32) for i in range(2)]
    dtt = ma("dtt", [128, 4, 16], F32)
    dAt = ma("dAt", [128, 4, 16], F32)
    Rm = ma("Rm", [128, 16, 128], F32)
    seg = sb("seg", [128, 16, 128], F32, addr=Rm.base)
    Pm = ma("Pm", [128, 16, 128], BF16)
    xdt = ma("xdt", [128, 16, 64], BF16)
    xsw = ma("xsw", [128, 16, 64], BF16)
    bmt = ma("bmt", [128, 256], BF16)
    ysb = ma("ysb", [128, 1024], F32)
    mcur[0] = mbase
    qT = ma("qT", [128, 8, TT], BF16)
    kT = ma("kT", [128, 8, TT], BF16)
    vt = ma("vt", [128, 4, 1024], BF16)
    sgT = ma("sgT", [128, 8, TT], BF16)
    ncs = [ma("ncs%d" % i, [128, TT], F32) for i in range(2)]
    rt = [ma("rt%d" % i, [128, TT], F32) for i in range(2)]
    kd = ma("kd", [128, 1024], BF16)
    Pt = ma("Pt", [128, 512], BF16)
    oa = ma("oa", [128, 1024], F32)
    ob = ma("ob", [128, 1024], F32)
    tmpT = sb("tmpT", [128, 8, 128], F32, addr=oa.base)
    posi = sb("posi", [128, TT], I32, addr=rt[1].base)

    psb = [nc.alloc_psum_tensor("ps%d" % b, [128, 512], F32) for b in range(8)]

    def PS(b, lo=0, hi=512, shape=None):
        ap = psb[b][:, lo:hi]
        if shape is not None:
            names = " ".join("a%d" % i for i in range(len(shape)))
            kw = {"a%d" % i: s for i, s in enumerate(shape)}
            ap = ap.rearrange("p (%s) -> p %s" % (names, names), **kw)
        return V(ap, "ps", b * 2048 + lo * 4, b * 2048 + hi * 4)

    def PSbf(b, lo=0, hi=1024, shape=None):
        ap = psb[b].bitcast(BF16)[:, lo:hi]
        if shape is not None:
            names = " ".join("a%d" % i for i in range(len(shape)))
            kw = {"a%d" % i: s for i, s in enumerate(shape)}
            ap = ap.rearrange("p (%s) -> p %s" % (names, names), **kw)
        return V(ap, "ps", b * 2048 + lo * 2, b * 2048 + hi * 2)

    pscur = [0]

    def psalloc(n=1):
        if pscur[0] % n:
            pscur[0] += n - pscur[0] % n
        if pscur[0] + n > 8:
            pscur[0] = 0
        b = pscur[0]
        pscur[0] += n
        return b

    def rs(v, shape):
        names = " ".join("a%d" % i for i in range(len(shape)))
        kw = {"a%d" % i: s for i, s in enumerate(shape)}
        return V(v.ap.rearrange("p (%s) -> p %s" % (names, names), **kw), v.space, v.lo, v.hi)

    def fl(v):
        nd = len(v.ap.shape) - 1
        names = " ".join("a%d" % i for i in range(nd))
        return V(v.ap.rearrange("p %s -> p (%s)" % (names, names)), v.space, v.lo, v.hi)

    def bc(v, axis, n):
        ap = v.ap.unsqueeze(axis)
        shp = list(ap.shape)
        shp[axis] = n
        return V(ap.broadcast_to(shp), v.space, v.lo, v.hi)

    def dma(eng, out, in_, slot, batch=False):
        S.op(eng, lambda e: e.dma_start(out=out.ap, in_=in_.ap), reads=[in_], writes=[out], slot=slot, batch=batch)

    def mm_group(out, pairs, reads, start=True, stop=True):
        def fn(e):
            n = len(pairs)
            ins = None
            for i, (l, r) in enumerate(pairs):
                ins = e.matmul(out.ap, lhsT=l, rhs=r, start=(start and i == 0), stop=(stop and i == n - 1))
            return ins
        S.op("pe", fn, reads=reads, writes=[out])

    def transp(out, in_, ident):
        S.op("pe", lambda e: e.transpose(out.ap, in_.ap, ident.ap), reads=[in_, ident], writes=[out])

    def act(out, in_, func, bias=None, scale=None):
        kw = {}
        rd = [in_]
        for nm, val in (("bias", bias), ("scale", scale)):
            if val is None:
                continue
            if isinstance(val, V):
                kw[nm] = val.ap
                rd.append(val)
            else:
                kw[nm] = val
        S.op("act", lambda e: e.activation(out=out.ap, in_=in_.ap, func=func, **kw), reads=rd, writes=[out])

    def tt(out, a, b, op, eng="dve"):
        S.op(eng, lambda e: e.tensor_tensor(out=out.ap, in0=a.ap, in1=b.ap, op=op), reads=[a, b], writes=[out])

    def ts(out, a, s1, s2, op0, op1=None, eng="dve"):
        rd = [a]
        v1 = s1.ap if isinstance(s1, V) else s1
        v2 = s2.ap if isinstance(s2, V) else s2
        if isinstance(s1, V):
            rd.append(s1)
        if isinstance(s2, V):
            rd.append(s2)
        if op1 is None:
            S.op(eng, lambda e: e.tensor_scalar(out=out.ap, in0=a.ap, scalar1=v1, scalar2=None, op0=op0),
                 reads=rd, writes=[out])
        else:
            S.op(eng, lambda e: e.tensor_scalar(out=out.ap, in0=a.ap, scalar1=v1, scalar2=v2, op0=op0, op1=op1),
                 reads=rd, writes=[out])

    def stt(out, a, s, b, op0, op1, eng="dve"):
        rd = [a, b]
        sv = s.ap if isinstance(s, V) else s
        if isinstance(s, V):
            rd.append(s)
        S.op(eng, lambda e: e.scalar_tensor_tensor(out=out.ap, in0=a.ap, scalar=sv, in1=b.ap, op0=op0, op1=op1),
             reads=rd, writes=[out])

    def copy(out, in_, eng="dve"):
        if eng == "act":
            act(out, in_, AF.Copy)
        else:
            S.op(eng, lambda e: e.tensor_copy(out=out.ap, in_=in_.ap), reads=[in_], writes=[out])

    def memset(v, val):
        S.op("dve", lambda e: e.memset(v.ap, val), reads=[], writes=[v])

    def bnstats(o, v):
        S.op("dve", lambda e: e.bn_stats(out=o.ap, in_=v.ap), reads=[v], writes=[o])

    def bnaggr(o, v):
        S.op("dve", lambda e: e.bn_aggr(out=o.ap, in_=v.ap), reads=[v], writes=[o])

    def rstd_from_var(var_v, tmp_v, out_v, eps):
        act(tmp_v, var_v, AF.Sqrt, bias=eps)
        S.op("dve", lambda e: e.reciprocal(out=out_v.ap, in_=tmp_v.ap), reads=[tmp_v], writes=[out_v])

    def allgather(in_buf, out_buf, groups, slot):
        iv, ov = in_buf.whole(), out_buf.whole()
        if hasattr(in_buf, "raw"):
            iv = V(in_buf.raw, iv.space, iv.lo, iv.hi)
            ov = V(out_buf.raw, ov.space, ov.lo, ov.hi)
        S.op("pool", lambda e: e.collective_compute("AllGather", ALU.bypass, replica_groups=groups,
                                                    ins=[iv.ap], outs=[ov.ap]),
             reads=[iv], writes=[ov], slot=slot, cc=True)

    ALL8 = [list(range(8))]
    SEQ4 = [[0, 1, 2, 3], [4, 5, 6, 7]]

    dma("sp", cst.whole(), cst_d.whole(), slot="cst")
    copy(ident_bf.whole(), C("ident"))
    for cc in range(8):
        ts(dmat[:, cc, :], C("ident"), C("dcol", cc), None, ALU.mult)
    act(a_bc.whole(), C("alog"), AF.Exp)
    ts(a_bc.whole(), a_bc.whole(), -1.0, None, ALU.mult)

    cast_n = {}

    def cast(dst_v, src_ap, slot):
        k = cast_n.get(dst_v.space, 0)
        cast_n[dst_v.space] = k + 1
        tok = V(dst_v.ap, dst_v.space, k, k + 1)
        S.op("pool", lambda e: e.dma_start(out=tok.ap, in_=src_ap), reads=[], writes=[tok], slot=slot, batch=True)

    def cast_ffn(i):
        parts = [(gu_l[i], gu_g[i], 0, 22)] if i else [(gu0a_l, gu0a_g, 0, GU_SPLIT), (gu0b_l, gu0b_g, GU_SPLIT, 22)]
        for pi_, (pl, pg, s0, s1) in enumerate(parts):
            lv = pl.h.rearrange("pp (s k dc f) -> pp s k dc f", s=s1 - s0, k=2, dc=16)
            for s_ in range(s0, s1):
                for k, src in ((0, wg_d[i]), (1, wu_d[i])):
                    sv = src.h.rearrange("dc pp (s f) -> pp s dc f", f=256)
                    cast(V(lv[:, s_ - s0, k], pl.space, 0, pl.nbytes), sv[:, s_], ("cgu", i, pi_))
            allgather(pl, pg, ALL8, ("ag_gu", i, pi_))
        lv = d_l[i].h.rearrange("pp (dr g fc n) -> pp dr g fc n", dr=4, g=4, fc=11)
        sv = wd_d[i].h.rearrange("(g fc) pp (dr n) -> pp dr g fc n", fc=11, n=512)
        for dr in range(4):
            for g in range(4):
                cast(V(lv[:, dr, g], d_l[i].space, 0, d_l[i].nbytes), sv[:, dr, g], ("cd", i))
        allgather(d_l[i], d_g[i], ALL8, ("ag_d", i))

    def cast_io():
        lv = io_l.h[:, 0:IO_IN].rearrange("pp (g dc n) -> pp g dc n", g=13, dc=16)
        sv = win_d.h[:, :, 0:6656].rearrange("dc pp (g n) -> pp g dc n", n=512)
        for g in range(13):
            cast(V(lv[:, g], io_l.space, 0, io_l.nbytes), sv[:, g], "cio")
        lv = io_l.h[:, IO_DT:IO_DT + 256].rearrange("pp (dc n) -> pp dc n", dc=16)
        sv = win_d.h[:, :, 6656:6672].rearrange("dc pp n -> pp dc n")
        cast(V(lv, io_l.space, 0, io_l.nbytes), sv, "cio")
        lv = io_l.h[:, IO_OUT:IO_OUT + 4 * 8192].rearrange("pp (dr mc n) -> pp dr mc n", dr=4, mc=16)
        sv = wout_d.h.rearrange("mc pp (dr n) -> pp dr mc n", n=512)
        for dr in range(4):
            cast(V(lv[:, dr], io_l.space, 0, io_l.nbytes), sv[:, dr], "cio")
        allgather(io_l, io_g, ALL8, "ag_io")

    cast_ffn(0)
    if stage >= 2:
        cast_io()
    if stage >= 3:
        cast_ffn(1)

    wslot = [0]

    def load_w(src_view, shape):
        i = wslot[0] % NW
        wslot[0] += 1
        nel = 1
        for s in shape[1:]:
            nel *= s
        dst = V(Wp[i].h[:, 0:nel], "sb", Wp[i].base, Wp[i].base + nel * 2)
        dma("sp", dst, src_view, slot=("W", i))
        view = Wp[i].h[:, 0:nel]
        if len(shape) > 2:
            names = " ".join("a%d" % k for k in range(len(shape) - 1))
            kw = {"a%d" % k: s for k, s in enumerate(shape[1:])}
            view = view.rearrange("p (%s) -> p %s" % (names, names), **kw)
        return view, dst

    def io_grp(g):
        return io_g[:, g * 8192:(g + 1) * 8192]

    def transpose_tile(src_tok, dstT):
        for dc in range(16):
            b = psalloc()
            for c in range(4):
                transp(PS(b, c * 128, (c + 1) * 128), src_tok[:, c, dc * 128:(dc + 1) * 128], C("ident"))
            copy(dstT[:, dc, :], PS(b), eng=("act" if dc % 2 == 0 else "dve"))

    def layer_norm(xt_buf, gbuf, ln_idx, eps):
        dma("sp", gbuf[0].whole(), lngb_d[2 * ln_idx], slot=("gb", 0))
        dma("sp", gbuf[1].whole(), lngb_d[2 * ln_idx + 1], slot=("gb", 1))
        for c in range(4):
            for q in range(4):
                bnstats(lnst[:, q * 6:(q + 1) * 6], xt_buf[:, c, q * 512:(q + 1) * 512])
            bnaggr(lnmv[:, c, 0:2], lnst.whole())
        rstd_from_var(lnmv[:, :, 1:2], lnmv[:, :, 2:3], lnmv[:, :, 3:4], eps)
        for c in range(4):
            ts(xt_buf[:, c, :], xt_buf[:, c, :], lnmv[:, c, 0:1], lnmv[:, c, 3:4], ALU.subtract, ALU.mult)
            tt(xt_buf[:, c, :], xt_buf[:, c, :], gbuf[0].whole(), ALU.mult)
            tt(xt_buf[:, c, :], xt_buf[:, c, :], gbuf[1].whole(), ALU.add)

    def ffn_tile(i, xTb, xt_buf, res_scale):
        for s in range(22):
            wv, wdst = load_w(gu_step(i, s), [128, 2, 16, 256])
            base = (s % 2) * 4
            for j in range(2):
                fc = 2 * s + j
                for k in range(2):
                    mm_group(PS(base + 2 * k + j),
                             [(wv[:, k, dc, j * 128:(j + 1) * 128], xTb[:, dc, :].ap) for dc in range(16)],
                             reads=[wdst, xTb.whole()])
                sg = sgt[fc % 2]
                act(sg.whole(), PS(base + j), AF.Silu)
                tt(hT[:, fc, :], sg.whole(), PS(base + 2 + j), ALU.mult)
        for dr in range(4):
            base = (dr % 2) * 4
            for g in range(4):
                o = (dr * 4 + g) * 5632
                wv, wdst = load_w(d_g[i][:, o:o + 5632], [128, 11, 512])
                for tcn in range(4):
                    mm_group(PS(base + tcn),
                             [(hT[:, g * 11 + f, tcn * 128:(tcn + 1) * 128].ap, wv[:, f, :]) for f in range(11)],
                             reads=[wdst, hT[:, g * 11:(g + 1) * 11, :]], start=(g == 0), stop=(g == 3))
            for tcn in range(4):
                v = xt_buf[:, tcn, dr * 512:(dr + 1) * 512]
                stt(v, PS(base + tcn), res_scale, v, ALU.mult, ALU.add)

    acum, tot, fs, te, et = (small[:, 0:16], small[:, 16:32], small[:, 32:48], small[:, 48:64], small[:, 64:80])
    gst = small[:, 96:120]
    gmv = sb("gmv", [128, 4, 4], F32, addr=small.base + 128 * 4)
    rst = small[:, 160:166]
    rmv = small[:, 168:176]
    tw = small[:, 176:192]
    hs = small[:, 192:228]

    def mixer_tile(t, xb, full):
        xw = xb.whole()
        wv, wd = load_w(io_g[:, IO_DT:IO_DT + 256], [128, 16, 16])
        b = psalloc()
        for c in range(4):
            mm_group(PS(b, c * 16, (c + 1) * 16),
                     [(xb[:, dc, c * 128:(c + 1) * 128].ap, wv[:, dc, :]) for dc in range(16)], reads=[wd, xw])
        tt(dtt.whole(), PS(b, 0, 64, shape=[4, 16]), bc(C("dtb"), 1, 4), ALU.add)
        act(dtt.whole(), dtt.whole(), AF.Exp)
        act(dtt.whole(), dtt.whole(), AF.Ln, bias=1.0)
        tt(dAt.whole(), dtt.whole(), bc(a_bc.whole(), 1, 4), ALU.mult)
        if full:
            for half in range(2):
                wv, wd = load_w(io_grp(8 + half), [128, 16, 512])
                for c in range(4):
                    b = psalloc()
                    mm_group(PS(b), [(xb[:, dc, c * 128:(c + 1) * 128].ap, wv[:, dc, :]) for dc in range(16)],
                             reads=[wd, xw])
                    act(sz[:, c, half * 512:(half + 1) * 512], PS(b), AF.Silu)
        for gi in range(3):
            wv, wd = load_w(io_grp(10 + gi), [128, 16, 512])
            for j in range(4):
                cc = gi * 4 + j
                b = psalloc()
                mm_group(PS(b), [(wv[:, dc, j * 128:(j + 1) * 128], xb[:, dc, :].ap) for dc in range(16)],
                         reads=[wd, xw])
                p, a = pre[cc % 2], acc[cc % 2]
                copy(p[:, 0:3], halo[:, cc, :], eng="act")
                copy(p[:, 3:TT + 3], PS(b), eng="act")
                ts(a.whole(), p[:, 0:TT], C("convw", cc * 4), None, ALU.mult)
                for k in range(1, 4):
                    stt(a.whole(), p[:, k:k + TT], C("convw", cc * 4 + k), a.whole(), ALU.mult, ALU.add)
                copy(halo[:, cc, :], p[:, TT:TT + 3], eng="act")
                act(xbcT[:, cc, :], a.whole(), AF.Silu, bias=C("convb", cc))
        for c in range(4):
            c0, c1_ = c * 128, (c + 1) * 128
            dAc = dAt[:, c, :]
            b = psalloc()
            mm_group(PS(b, 0, 16), [(C("U").ap, dAc.ap)], reads=[C("U"), dAc])
            mm_group(PS(b, 16, 32), [(C("ones").ap, dAc.ap)], reads=[C("ones"), dAc])
            copy(small[:, 0:32], PS(b, 0, 32), eng="act")
            act(fs, acum, AF.Exp)
            tt(te, tot, acum, ALU.subtract)
            act(te, te, AF.Exp)
            act(et, tot, AF.Exp)
            if not full:
                tt(atot.whole(), atot.whole(), tot, ALU.add)
            b = psalloc()
            for cc in range(8):
                transp(PSbf(b, cc * 128, (cc + 1) * 128), xbcT[:, cc, c0:c1_], ident_bf.whole())
            tt(xdt.whole(), PSbf(b, shape=[16, 64]), bc(dtt[:, c, :], 2, 64), ALU.mult)
            tt(xsw.whole(), xdt.whole(), bc(te, 2, 64), ALU.mult)
            b = psalloc()
            for g in range(2):
                transp(PSbf(b, g * 128, (g + 1) * 128), xbcT[:, 8 + g, c0:c1_], ident_bf.whole())
            copy(bmt.whole(), PSbf(b, 0, 256), eng="act")
            if full:
                tt(Rm.whole(), bc(C("U"), 1, 16), bc(dAc, 2, 128), ALU.mult)
                b4 = psalloc(4)
                for q in range(4):
                    rq = fl(Rm[:, q * 4:(q + 1) * 4, :])
                    mm_group(PS(b4 + q), [(C("ones").ap, rq.ap)], reads=[C("ones"), rq])
                for q in range(4):
                    tt(seg[:, q * 4:(q + 1) * 4, :], PS(b4 + q, shape=[4, 128]),
                       bc(small[:, q * 4:(q + 1) * 4], 2, 128), ALU.subtract)
                tt(seg.whole(), seg.whole(), bc(C("cmask"), 1, 16), ALU.add)
                act(seg.whole(), seg.whole(), AF.Exp)
                b = psalloc()
                for g in range(2):
                    bm, cm = xbcT[:, 8 + g, c0:c1_], xbcT[:, 10 + g, c0:c1_]
                    mm_group(PS(b, g * 128, (g + 1) * 128), [(bm.ap, cm.ap)], reads=[bm, cm])
                for g in range(2):
                    tt(Pm[:, g * 8:(g + 1) * 8, :], seg[:, g * 8:(g + 1) * 8, :],
                       bc(PS(b, g * 128, (g + 1) * 128), 1, 8), ALU.mult)
                by = psalloc(2)
                for h in range(16):
                    cc, half = h // 2, h % 2
                    col = (h % 8) * 64
                    xc = xbcT[:, cc, c0:c1_]
                    dm = dmat[:, cc, half * 64:(half + 1) * 64]
                    mm_group(PS(by + h // 8, col, col + 64), [(xc.ap, dm.ap), (Pm[:, h, :].ap, xdt[:, h, :].ap)],
                             reads=[xc, dm, Pm[:, h, :], xdt[:, h, :]])
                bi = psalloc(2)
                for g in range(2):
                    cm, sg_ = xbcT[:, 10 + g, c0:c1_], st_b[:, g * 512:(g + 1) * 512]
                    mm_group(PS(bi + g), [(cm.ap, sg_.ap)], reads=[cm, sg_])
                for g in range(2):
                    yv = ysb[:, g * 512:(g + 1) * 512]
                    tt(rs(yv, [8, 64]), PS(bi + g, shape=[8, 64]), bc(small[:, 32 + g * 8:32 + (g + 1) * 8], 2, 64),
                       ALU.mult)
                    tt(yv, yv, PS(by + g), ALU.add)
                tt(ysb.whole(), ysb.whole(), sz[:, c, :], ALU.mult)
                for g in range(2):
                    yv = ysb[:, g * 512:(g + 1) * 512]
                    bnstats(rst, yv)
                    bnaggr(rmv[:, 0:2] if False else small[:, 168:170], rst)
                    stt(small[:, 170:171], small[:, 168:169], small[:, 168:169], small[:, 169:170], ALU.mult, ALU.add)
                    rstd_from_var(small[:, 170:171], small[:, 171:172], small[:, 172 + g:173 + g], EPS)
                    ts(yv, yv, small[:, 172 + g:173 + g], None, ALU.mult)
                for half in range(2):
                    b = psalloc()
                    for j in range(4):
                        cc = half * 4 + j
                        transp(PS(b, j * 128, (j + 1) * 128), ysb[:, cc * 128:(cc + 1) * 128], C("ident"))
                    for j in range(4):
                        cc = half * 4 + j
                        act(mixT[:, 8 + cc, c0:c1_], PS(b, j * 128, (j + 1) * 128), AF.Identity, scale=C("ng", cc))
            bs = psalloc(2)
            for g in range(2):
                bmv, xv = bmt[:, g * 128:(g + 1) * 128], fl(xsw[:, g * 8:(g + 1) * 8, :])
                mm_group(PS(bs + g), [(bmv.ap, xv.ap)], reads=[bmv, xv])
            for g in range(2):
                sv = st_f[:, g * 512:(g + 1) * 512]
                tt(rs(sv, [8, 64]), rs(sv, [8, 64]), bc(small[:, 64 + g * 8:64 + (g + 1) * 8], 2, 64), ALU.mult)
                tt(sv, sv, PS(bs + g), ALU.add)
            copy(st_b.whole(), st_f.whole(), eng="act")

        dma("sp", posi.whole(), pos_d[:, t * TT:(t + 1) * TT], slot="posi")
        th, kf = rt[0].whole(), ncs[0].whole()
        copy(th, posi.whole())
        ts(th, th, C("invf"), None, ALU.mult)
        ts(kf, th, 1.0 / TWO_PI, None, ALU.mult)
        copy(posi.whole(), kf)
        copy(kf, posi.whole())
        stt(th, kf, -CW1, th, ALU.mult, ALU.add)
        stt(th, kf, -CW2, th, ALU.mult, ALU.add)
        ts(th, th, 3.141592, -3.141592, ALU.min, ALU.max)
        sinv, cosv = ncs[1].whole(), ncs[0].whole()
        act(sinv, th, AF.Sin)
        ts(rt[1].whole(), th, -1.0, None, ALU.mult)
        tt(th, th, rt[1].whole(), ALU.max)
        act(cosv, th, AF.Sin, scale=-1.0, bias=math.pi / 2.0)
        for (isq, dst, grp0) in ((True, qT, 0), (False, kT, 2)):
            if isq and not full:
                continue
            for gi in range(2):
                wv, wd = load_w(io_grp(grp0 + gi), [128, 16, 512])
                bb = psalloc(4)
                for j in range(4):
                    mm_group(PS(bb + j), [(wv[:, dc, j * 128:(j + 1) * 128], xb[:, dc, :].ap) for dc in range(16)],
                             reads=[wd, xw])
                for hh in range(2):
                    pe_, po_ = PS(bb + 2 * hh), PS(bb + 2 * hh + 1)
                    ce = gi * 4 + 2 * hh
                    r0, r1 = rt[0].whole(), rt[1].whole()
                    tt(r0, pe_, cosv, ALU.mult)
                    tt(r1, po_, sinv, ALU.mult)
                    tt(dst[:, ce, :], r0, r1, ALU.subtract)
                    tt(r0, po_, cosv, ALU.mult)
                    tt(r1, pe_, sinv, ALU.mult)
                    tt(dst[:, ce + 1, :], r0, r1, ALU.add)
        for half in range(2):
            wv, wd = load_w(io_grp(4 + half), [128, 16, 512])
            for c in range(4):
                b = psalloc()
                mm_group(PS(b), [(xb[:, dc, c * 128:(c + 1) * 128].ap, wv[:, dc, :]) for dc in range(16)],
                         reads=[wd, xw])
                copy(vt[:, c, half * 512:(half + 1) * 512], PS(b), eng="act")
        if full:
            for gi in range(2):
                wv, wd = load_w(io_grp(6 + gi), [128, 16, 512])
                for j in range(4):
                    b = psalloc()
                    mm_group(PS(b), [(wv[:, dc, j * 128:(j + 1) * 128], xb[:, dc, :].ap) for dc in range(16)],
                             reads=[wd, xw])
                    act(sgT[:, gi * 4 + j, :], PS(b), AF.Silu)
        for c in range(4):
            c0, c1_ = c * 128, (c + 1) * 128
            b = psalloc()
            for kc in range(8):
                transp(PSbf(b, kc * 128, (kc + 1) * 128), kT[:, kc, c0:c1_], ident_bf.whole())
            for h in range(4):
                ts(kd[:, h * 256:(h + 1) * 256], PSbf(b, h * 256, (h + 1) * 256), C("kdec", h), None, ALU.mult)
            if full:
                b = psalloc()
                for h in range(4):
                    kk, qq = kT[:, 2 * h:2 * h + 2, c0:c1_], qT[:, 2 * h:2 * h + 2, c0:c1_]
                    mm_group(PS(b, h * 128, (h + 1) * 128),
                             [(kT[:, 2 * h + e, c0:c1_].ap, qT[:, 2 * h + e, c0:c1_].ap) for e in range(2)],
                             reads=[kk, qq])
                tt(Pt.whole(), PS(b), C("dmask"), ALU.mult)
                ba = psalloc(2)
                bo = psalloc(2)
                for h in range(4):
                    col = (h % 2) * 256
                    pv, vv = Pt[:, h * 128:(h + 1) * 128], vt[:, c, h * 256:(h + 1) * 256]
                    mm_group(PS(ba + h // 2, col, col + 256), [(pv.ap, vv.ap)], reads=[pv, vv])
                    qq, ss = qT[:, 2 * h:2 * h + 2, c0:c1_], S_b[:, 2 * h * 256:(2 * h + 2) * 256]
                    mm_group(PS(bo + h // 2, col, col + 256),
                             [(qT[:, 2 * h + e, c0:c1_].ap, S_b[:, (2 * h + e) * 256:(2 * h + e + 1) * 256].ap)
                              for e in range(2)], reads=[qq, ss])
                for g2 in range(2):
                    copy(oa[:, g2 * 512:(g2 + 1) * 512], PS(ba + g2), eng="act")
                for h in range(4):
                    col = (h % 2) * 256
                    stt(ob[:, h * 256:(h + 1) * 256], PS(bo + h // 2, col, col + 256), C("qdec", h),
                        oa[:, h * 256:(h + 1) * 256], ALU.mult, ALU.add)
                for h in range(4):
                    bnstats(small[:, 96 + h * 6:102 + h * 6], ob[:, h * 256:(h + 1) * 256])
                    bnaggr(gmv[:, h, 0:2], small[:, 96 + h * 6:102 + h * 6])
                rstd_from_var(gmv[:, :, 1:2], gmv[:, :, 2:3], gmv[:, :, 3:4], EPS)
                for h in range(4):
                    ov = ob[:, h * 256:(h + 1) * 256]
                    ts(ov, ov, gmv[:, h, 0:1], gmv[:, h, 3:4], ALU.subtract, ALU.mult)
                for half in range(2):
                    b = psalloc()
                    for j in range(4):
                        mc = half * 4 + j
                        transp(PS(b, j * 128, (j + 1) * 128), ob[:, mc * 128:(mc + 1) * 128], C("ident"))
                    for j in range(4):
                        mc = half * 4 + j
                        act(tmpT[:, mc, :], PS(b, j * 128, (j + 1) * 128), AF.Identity, scale=C("gng", mc),
                            bias=C("gnb", mc))
                tt(mixT[:, 0:8, c0:c1_], tmpT.whole(), sgT[:, :, c0:c1_], ALU.mult)
            bs = psalloc(4)
            for h in range(4):
                for e in range(2):
                    idx = 2 * h + e
                    col = (idx % 2) * 256
                    kv, vv = kd[:, idx * 128:(idx + 1) * 128], vt[:, c, h * 256:(h + 1) * 256]
                    mm_group(PS(bs + idx // 2, col, col + 256), [(kv.ap, vv.ap)], reads=[kv, vv])
            for h in range(4):
                sv = S_f[:, h * 512:(h + 1) * 512]
                stt(sv, sv, GAM[h] ** 128.0, PS(bs + h), ALU.mult, ALU.add)
            copy(S_b.whole(), S_f.whole(), eng="act")

    c1 = 0.5 / ALPHA
    eps1 = EPS / (ALPHA * ALPHA)
    for n, t in enumerate(tiles_a1):
        xb = xT[n % 2]
        dma("sp", xtok.whole(), V(x_d.h[t * TT:(t + 1) * TT, :].rearrange("(c p) d -> p c d", p=128), "x", t, t + 1),
            slot="xtok")
        transpose_tile(xtok, xb)
        if DBG != 1:
            ffn_tile(0, xb, xtok, c1)
        if DBG not in (1, 2):
            layer_norm(xtok, gb[n % 2], 0, eps1)
        if stage == 1:
            dma("sp", V(out_d.h[t * TT:(t + 1) * TT, :].rearrange("(c p) d -> p c d", p=128), "out", t, t + 1),
                xtok.whole(), slot="xtok_st")
            continue
        dma("sp", x1s[t], xtok.whole(), slot="xtok_st")
        transpose_tile(xtok, x1T_stage)
        dma("sp", x1Ts[t], x1T_stage.whole(), slot="x1T_st")
        if t == 3:
            copy(xl3.whole(), x1T_stage[:, :, TT - 3:TT])
        if n == 1 and DBG != 5:
            b = psalloc()
            for gi in range(3):
                wv, wd = load_w(io_grp(10 + gi), [128, 16, 512])
                for j in range(4):
                    cc = gi * 4 + j
                    mm_group(PS(b, cc * 3, cc * 3 + 3),
                             [(wv[:, dc, j * 128:(j + 1) * 128], xl3[:, dc, :].ap) for dc in range(16)],
                             reads=[wd, xl3.whole()])
            copy(hs, PS(b, 0, 36))
            dma("sp", hsend.whole(), hs, slot="hs")
            allgather(hsend, hrecv, SEQ4, "ag_h")

    if stage >= 2 and DBG != 5:
        dma("sp", hr.whole(), V(hrecv.h.rearrange("(r p) n -> p r n", p=128), hrecv.space, 0, hrecv.nbytes), slot="hr")
        h0 = fl(halo0.whole())
        ts(h0, hr[:, 0, :], C("ohalo", 0), None, ALU.mult)
        for r in range(1, 4):
            stt(h0, hr[:, r, :], C("ohalo", r), h0, ALU.mult, ALU.add)
        for v in (S_f, S_b, st_f, st_b, atot):
            memset(v.whole(), 0.0)
        copy(halo.whole(), halo0.whole())
        for t in range(NT if DBG != 3 else 0):
            xb = xT[t % 2]
            dma("sp", xb.whole(), x1Ts[t], slot=("xT", t % 2))
            mixer_tile(t, xb, False)
    if stage >= 2 and DBG not in (3, 4, 5):
        dma("sp", ssend[:, 0:2048], S_f.whole(), slot="ss0")
        dma("sp", ssend[:, 2048:3072], st_f.whole(), slot="ss1")
        dma("sp", ssend[:, 3072:3088], atot.whole(), slot="ss2")
        allgather(ssend, srecv, ALL8, "ag_s")
        for v in (S_f, st_f):
            memset(v.whole(), 0.0)
        for i_, r in enumerate((0, 1, 2, 4, 5, 6)):
            rb_ = (rbuf, rbuf2)[i_ % 2]
            dma("sp", rb_.whole(), srecv[r * 128:(r + 1) * 128, :], slot=("rbuf", i_ % 2))
            ts(rb_[:, 0:3072], rb_[:, 0:3072], C("mr", r), None, ALU.mult)
            for h in range(4):
                sv = S_f[:, h * 512:(h + 1) * 512]
                stt(sv, sv, C("retw", r * 4 + h), rb_[:, h * 512:(h + 1) * 512], ALU.mult, ALU.add)
            ts(tw, rb_[:, 3072:3088], C("mr", r), None, ALU.mult)
            act(tw, tw, AF.Exp)
            for g in range(2):
                sv = st_f[:, g * 512:(g + 1) * 512]
                tt(rs(sv, [8, 64]), rs(sv, [8, 64]), bc(small[:, 176 + g * 8:176 + (g + 1) * 8], 2, 64), ALU.mult)
                tt(sv, sv, rb_[:, 2048 + g * 512:2048 + (g + 1) * 512], ALU.add)
        copy(S_b.whole(), S_f.whole(), eng="act")
        copy(st_b.whole(), st_f.whole(), eng="act")
        copy(halo.whole(), halo0.whole())

    if stage >= 3:
        for t in range(NT):
            xb, xo = xT[0], xT[1]
            dma("sp", xb.whole(), x1Ts[t], slot=("xT", 0))
            mixer_tile(t, xb, True)
            dma("sp", xtok.whole(), x1s[t], slot="xtok")
            for dr in range(4):
                wv, wd = load_w(io_g[:, IO_OUT + dr * 8192:IO_OUT + (dr + 1) * 8192], [128, 16, 512])
                for tcn in range(4):
                    b = psalloc()
                    mm_group(PS(b), [(mixT[:, mc, tcn * 128:(tcn + 1) * 128].ap, wv[:, mc, :]) for mc in range(16)],
                             reads=[wd, mixT.whole()])
                    v = xtok[:, tcn, dr * 512:(dr + 1) * 512]
                    stt(v, PS(b), 1.0 / ALPHA, v, ALU.mult, ALU.add)
            layer_norm(xtok, gb[1], 1, eps1)
            transpose_tile(xtok, xo)
            ffn_tile(1, xo, xtok, c1)
            layer_norm(xtok, gb[0], 2, eps1)
            dma("sp", V(out_d.h[t * TT:(t + 1) * TT, :].rearrange("(c p) d -> p c d", p=128), "out", t, t + 1),
                xtok.whole(), slot="xtok_st")
    elif stage == 2:
        dma("sp", V(out_d.h[0:128, :], "out", 0, 1), S_f.whole(), slot="dbg0")
        dma("sp", V(out_d.h[128:256, 0:1024], "out", 1, 2), st_f.whole(), slot="dbg1")

    S.emit(nc)
    es.close()
    return nc


def _perm_cols():
    idx = np.arange(DIN)
    for base in (0, 1024):
        for h in range(4):
            o = base + h * 256
            idx[o:o + 128] = o + np.arange(0, 256, 2)
            idx[o + 128:o + 256] = o + np.arange(1, 256, 2)
    return idx


def make_in_maps(x, positions, ffn1_w_gate, ffn1_w_up, ffn1_w_down, ln1_gain, ln1_bias, mix_w_in, ret_gn_gain,
                 ret_gn_bias, ssd_conv_w, ssd_conv_b, ssd_dt_bias, ssd_a_log, ssd_d, ssd_norm_gain, mix_w_out,
                 ln2_gain, ln2_bias, ffn2_w_gate, ffn2_w_up, ffn2_w_down, ln3_gain, ln3_bias):
    A = lambda a: np.asarray(a, np.float32)
    win = A(mix_w_in)[0][:, _perm_cols()]
    lngb = np.ascontiguousarray(np.stack([np.broadcast_to(A(v).reshape(1, D), (128, D)) for v in
                                          (ln1_gain, ln1_bias, ln2_gain, ln2_bias, ln3_gain, ln3_bias)]))
    full = {"wg1": A(ffn1_w_gate)[0], "wu1": A(ffn1_w_up)[0], "wd1": A(ffn1_w_down)[0],
            "wg2": A(ffn2_w_gate)[0], "wu2": A(ffn2_w_up)[0], "wd2": A(ffn2_w_down)[0],
            "win": win, "wout": A(mix_w_out)[0]}
    xs = A(x)
    ps = np.asarray(positions, np.int32)
    maps = []
    for c in range(8):
        b, r = c // 4, c % 4
        m = {"lngb": lngb}
        for k, w in full.items():
            rows, cols = w.shape
            m[k] = np.ascontiguousarray(w.reshape(rows // 128, 128, cols)[:, 16 * c:16 * (c + 1), :])
        m["x"] = np.ascontiguousarray(xs[b, r * TOK:(r + 1) * TOK, :])
        m["pos"] = np.ascontiguousarray(np.broadcast_to(ps[b, r * TOK:(r + 1) * TOK][None, :], (128, TOK)))
        m["cst"] = _host_consts(c, A(ssd_conv_w)[0], A(ssd_conv_b)[0], A(ret_gn_gain)[0], A(ret_gn_bias)[0],
                                A(ssd_norm_gain)[0], A(ssd_d)[0], A(ssd_dt_bias)[0], A(ssd_a_log)[0])
        maps.append(m)
    return maps


_NC_CACHE = {}


def kernel(**inputs):
    stage = 3
    if stage not in _NC_CACHE:
        _NC_CACHE[stage] = build(stage)
    nc = _NC_CACHE[stage]
    maps = make_in_maps(**inputs)
    res = run_bass_kernel_spmd(nc, maps, core_ids=list(range(8)), trace=True)
    out = np.empty((2, 8192, D), np.float32)
    for c in range(8):
        b, r = c // 4, c % 4
        out[b, r * TOK:(r + 1) * TOK, :] = np.asarray(res.results[c]["out"], np.float32)
    return out
```

```python
import math
import os
DBG = int(os.environ.get('KDBG', '0'))
import contextlib
import numpy as np
import concourse.bass as bass
import concourse.mybir as mybir
from concourse.bass_utils import run_bass_kernel_spmd

F32 = mybir.dt.float32
BF16 = mybir.dt.bfloat16
I32 = mybir.dt.int32
AF = mybir.ActivationFunctionType
ALU = mybir.AluOpType

D = 2048
DFF = 5632
NFC = 44
TOK = 2048
TT = 512
NT = TOK // TT
CH = 128
DIN = 6672
ALPHA = 2.0 ** 0.25
EPS = 1e-5
GAM = [1.0 - 2.0 ** (-5.0 - h) for h in range(4)]

SB_LO = 16384 + 1024
SB_HI = 229376 - 256


class V:
    __slots__ = ("ap", "space", "lo", "hi")

    def __init__(self, ap, space, lo, hi):
        self.ap, self.space, self.lo, self.hi = ap, space, lo, hi


class Buf:
    def __init__(self, h, space, base, shape, esz, pdim=True):
        self.h, self.space, self.base, self.shape, self.esz, self.pdim = h, space, base, list(shape), esz, pdim
        st = [1] * len(shape)
        for i in range(len(shape) - 2, -1, -1):
            st[i] = st[i + 1] * shape[i + 1]
        self.st = st
        self.nbytes = (st[1] * shape[1] if pdim else st[0] * shape[0]) * esz

    def __getitem__(self, idx):
        if not isinstance(idx, tuple):
            idx = (idx,)
        idx = tuple(idx) + (slice(None),) * (len(self.shape) - len(idx))
        lo = 0
        hi = 0
        for d, (i, n, s) in enumerate(zip(idx, self.shape, self.st)):
            if self.pdim and d == 0:
                continue
            if isinstance(i, slice):
                a = 0 if i.start is None else i.start
                b = n if i.stop is None else i.stop
            else:
                a, b = i, i + 1
            lo += a * s
            hi += (b - 1) * s
        return V(self.h[idx], self.space, self.base + lo * self.esz, self.base + (hi + 1) * self.esz)

    def whole(self):
        return self[tuple(slice(None) for _ in self.shape)]

    def cust(self, ap, lo_el, hi_el):
        return V(ap, self.space, self.base + lo_el * self.esz, self.base + hi_el * self.esz)


class Sched:
    ENG = ("pe", "act", "dve", "pool", "sp")

    def __init__(self):
        self.ops = {e: [] for e in self.ENG}
        self.Wr = {}
        self.Rd = {}
        self.slot_count = {}
        self.slot_batch = {}

    @staticmethod
    def _rng(v):
        if v.space == "ps":
            b = v.lo // 2048
            return b * 2048, (b + 1) * 2048
        return v.lo, v.hi

    def op(self, eng, fn, reads=(), writes=(), slot=None, batch=False, cc=False):
        deps = {}

        def add(ref):
            k = ref[:2]
            if deps.get(k, -1) < ref[2]:
                deps[k] = ref[2]

        for v in reads:
            vlo, vhi = self._rng(v)
            for (lo, hi, ref) in self.Wr.get(v.space, ()):
                if lo < vhi and vlo < hi:
                    add(ref)
            if v.space == "ps":
                for (lo, hi, k), ref in self.Rd.get(v.space, {}).items():
                    if lo < vhi and vlo < hi and k != ("c", eng):
                        add(ref)
        for v in writes:
            vlo, vhi = self._rng(v)
            for (lo, hi, ref) in self.Wr.get(v.space, ()):
                if lo < vhi and vlo < hi:
                    add(ref)
            for (lo, hi, _k), ref in self.Rd.get(v.space, {}).items():
                if lo < vhi and vlo < hi:
                    add(ref)
        idx = len(self.ops[eng])
        if slot is not None:
            c = self.slot_count.get(slot, 0) + 1
            self.slot_count[slot] = c
            self.slot_batch[slot] = (batch, cc)
            ref = ("d", slot, c)
        else:
            ref = ("c", eng, idx)
        self.ops[eng].append(dict(fn=fn, deps=deps, slot=slot, ref=ref, needed=False))
        for v in reads:
            vlo, vhi = self._rng(v)
            self.Rd.setdefault(v.space, {})[(vlo, vhi, ref[:2])] = ref
        for v in writes:
            vlo, vhi = self._rng(v)
            rd = self.Rd.get(v.space)
            if rd:
                for k in [k for k in rd if vlo <= k[0] and k[1] <= vhi]:
                    del rd[k]
            wl = self.Wr.setdefault(v.space, [])
            wl[:] = [w for w in wl if not (vlo <= w[0] and w[1] <= vhi)]
            wl.append((vlo, vhi, ref))

    def check_deadlock(self, same_eng_sync=True):
        sem = {}
        pc = {e: 0 for e in self.ENG}
        progress = True
        while progress:
            progress = False
            for eng in self.ENG:
                ops = self.ops[eng]
                while pc[eng] < len(ops):
                    o = ops[pc[eng]]
                    ok = True
                    for (kind, name), val in o["deps"].items():
                        if kind == "c":
                            if name == eng and (eng == "pe" or not same_eng_sync):
                                continue
                            key, target = ("c", name), self.ops[name][val]["val"]
                        else:
                            batch, cc = self.slot_batch[name]
                            n = self.slot_count[name] if batch else val
                            key, target = ("d", name), n
                        if sem.get(key, 0) < target:
                            ok = False
                            break
                    if not ok:
                        break
                    if o["slot"] is not None:
                        sem[("d", o["slot"])] = sem.get(("d", o["slot"]), 0) + 1
                    elif o["needed"]:
                        sem[("c", eng)] = sem.get(("c", eng), 0) + 1
                    pc[eng] += 1
                    progress = True
        stuck = {e: (pc[e], len(self.ops[e])) for e in self.ENG if pc[e] < len(self.ops[e])}
        if stuck:
            for e, (p, n) in stuck.items():
                print("DEADLOCK", e, p, n, self.ops[e][p]["deps"], self.ops[e][p]["ref"])
            raise RuntimeError("semaphore program deadlocks: %s" % stuck)
        print("deadlock check ok:", {e: len(self.ops[e]) for e in self.ENG}, "max sem", max(sem.values()))

    def emit(self, nc, same_eng_sync=True):
        for eng, ops in self.ops.items():
            for o in ops:
                for (kind, name), val in o["deps"].items():
                    if kind == "c":
                        if name == eng and (eng == "pe" or not same_eng_sync):
                            continue
                        self.ops[name][val]["needed"] = True
        for eng, ops in self.ops.items():
            c = 0
            for o in ops:
                if o["needed"]:
                    c += 1
                o["val"] = c
        self.check_deadlock(same_eng_sync)
        with contextlib.ExitStack() as es:
            esem = {e: es.enter_context(nc.semaphore("se_" + e)) for e in ("pe", "act", "dve", "pool")}
            ssem = {s: es.enter_context(nc.semaphore("sd_%d" % i)) for i, s in enumerate(self.slot_count)}
            block = es.enter_context(nc.Block())
            sched = self

            def run(eng, e):
                seen = {}
                for o in sched.ops[eng]:
                    for (kind, name), val in o["deps"].items():
                        if kind == "c":
                            if name == eng and (eng == "pe" or not same_eng_sync):
                                continue
                            sem, target = esem[name], sched.ops[name][val]["val"]
                        else:
                            batch, cc = sched.slot_batch[name]
                            n = sched.slot_count[name] if batch else val
                            sem, target = ssem[name], (n if cc else 16 * n)
                        if seen.get((kind, name), 0) >= target:
                            continue
                        seen[(kind, name)] = target
                        e.wait_ge(sem, target)
                    ins = o["fn"](e)
                    if o["slot"] is not None:
                        if sched.slot_batch[o["slot"]][1]:
                            ins.then_inc(ssem[o["slot"]])
                        else:
                            ins.then_inc(ssem[o["slot"]], 16)
                    elif o["needed"]:
                        ins.then_inc(esem[eng], 1)
                for s, n in sched.slot_count.items():
                    if sched.slot_owner.get(s) == eng:
                        cc = sched.slot_batch[s][1]
                        e.wait_ge(ssem[s], n if cc else 16 * n)

            self.slot_owner = {}
            for eng, ops in self.ops.items():
                for o in ops:
                    if o["slot"] is not None:
                        self.slot_owner[o["slot"]] = eng

            @block.tensor
            def _(e):
                run("pe", e)

            @block.scalar
            def _(e):
                run("act", e)

            @block.vector
            def _(e):
                run("dve", e)

            @block.gpsimd
            def _(e):
                run("pool", e)

            @block.sync
            def _(e):
                run("sp", e)


_off = {}
_n = 0
for _name, _w in (("ident", 128), ("U", 128), ("ones", 128), ("cmask", 128), ("dmask", 512), ("qdec", 4),
                  ("kdec", 4), ("invf", 1), ("convw", 48), ("convb", 12), ("gng", 8), ("gnb", 8), ("ng", 8),
                  ("dcol", 8), ("dtb", 16), ("alog", 16), ("ohalo", 4), ("mr", 7), ("retw", 28)):
    _off[_name] = (_n, _n + _w)
    _n += _w
NCST = _n


def _host_consts(core, ssd_conv_w, ssd_conv_b, ret_gn_gain, ret_gn_bias, ssd_norm_gain, ssd_d, ssd_dt_bias, ssd_a_log):
    c = np.zeros((128, NCST), np.float32)

    def put(name, arr):
        a, b = _off[name]
        c[:, a:b] = np.asarray(arr, np.float32).reshape(128, b - a)

    p = np.arange(128)
    put("ident", np.eye(128))
    put("U", (p[:, None] <= p[None, :]).astype(np.float32))
    put("ones", np.ones((128, 128)))
    put("cmask", np.where(p[None, :] >= p[:, None], 0.0, -30000.0))
    dm = np.zeros((128, 4, 128), np.float64)
    for h in range(4):
        rel = (p[None, :] - p[:, None]).astype(np.float64)
        dm[:, h, :] = np.where(rel >= 0, GAM[h] ** np.maximum(rel, 0.0), 0.0) / 16.0
    put("dmask", dm)
    put("qdec", np.stack([GAM[h] ** (p + 1.0) for h in range(4)], axis=1))
    put("kdec", np.stack([GAM[h] ** (127.0 - p) / 16.0 for h in range(4)], axis=1))
    invf = (1.0 / (np.float32(10000.0) ** np.linspace(0.0, 1.0, 128, dtype=np.float32))).astype(np.float32)
    put("invf", invf)
    put("convw", ssd_conv_w.reshape(4, 12, 128).transpose(2, 1, 0))
    put("convb", ssd_conv_b.reshape(12, 128).T)
    put("gng", ret_gn_gain.reshape(8, 128).T)
    put("gnb", ret_gn_bias.reshape(8, 128).T)
    put("ng", ssd_norm_gain.reshape(8, 128).T)
    put("dcol", np.repeat(ssd_d.reshape(16), 64).reshape(8, 128).T)
    put("dtb", np.broadcast_to(ssd_dt_bias.reshape(1, 16), (128, 16)))
    put("alog", np.broadcast_to(ssd_a_log.reshape(1, 16), (128, 16)))
    rank, seq = core % 4, core // 4
    oh = np.zeros(4)
    if rank > 0:
        oh[rank - 1] = 1.0
    put("ohalo", np.broadcast_to(oh, (128, 4)))
    mr = np.array([1.0 if (r8 // 4 == seq and r8 % 4 < rank) else 0.0 for r8 in range(7)])
    put("mr", np.broadcast_to(mr, (128, 7)))
    rw = np.array([[(GAM[h] ** 2048.0) if mr[r8] > 0 else 1.0 for h in range(4)] for r8 in range(7)]).reshape(28)
    put("retw", np.broadcast_to(rw, (128, 28)))
    return c


GU_W = 22 * 8192
D_W = 16 * 5632
IO_IN = 13 * 8192
IO_DT = IO_IN
IO_OUT = IO_IN + 256
IO_W = 69 * 2048
NST = 2048 + 1024 + 16
TWO_PI = 2.0 * math.pi
CW1 = 6.28125
CW2 = TWO_PI - CW1


def build(stage=3, tiles_a1=(3, 0, 1, 2)):
    nc = bass.Bass("TRN2", target_bir_lowering=False)
    S = Sched()
    es = contextlib.ExitStack()
    es.enter_context(nc.allow_low_precision("bf16 matmul operands, fp32 accumulation"))

    def dram_in(name, shape, dt):
        t = nc.dram_tensor(name, shape, dt, kind="ExternalInput")
        return Buf(t.ap(), name, 0, shape, 4 if dt != BF16 else 2, pdim=False)

    def dram_tmp(name, shape, dt):
        t = nc.dram_tensor(name, shape, dt)
        return Buf(t.ap(), name, 0, shape, 4 if dt != BF16 else 2, pdim=False)

    x_d = dram_in("x", [TOK, D], F32)
    pos_d = dram_in("pos", [128, TOK], I32)
    cst_d = dram_in("cst", [128, NCST], F32)
    lngb_d = dram_in("lngb", [6, 128, D], F32)
    wg_d = [dram_in("wg%d" % i, [16, 16, DFF], F32) for i in (1, 2)]
    wu_d = [dram_in("wu%d" % i, [16, 16, DFF], F32) for i in (1, 2)]
    wd_d = [dram_in("wd%d" % i, [NFC, 16, D], F32) for i in (1, 2)]
    win_d = dram_in("win", [16, 16, DIN], F32)
    wout_d = dram_in("wout", [16, 16, D], F32)
    out_t = nc.dram_tensor("out", [TOK, D], F32, kind="ExternalOutput")
    out_d = Buf(out_t.ap(), "out", 0, [TOK, D], 4, pdim=False)

    def ag_pair(name, width):
        j = width // 2048
        tl = nc.dram_tensor(name + "_l", [16 * j, 2048], BF16)
        tg = nc.dram_tensor(name + "_g", [128 * j, 2048], BF16)
        bl = Buf(tl.ap().rearrange("(pp j) n -> pp (j n)", pp=16), name + "_l", 0, [16, width], 2, pdim=False)
        bg = Buf(tg.ap().rearrange("(pp j) n -> pp (j n)", pp=128), name + "_g", 0, [128, width], 2, pdim=False)
        bl.raw, bg.raw = tl.ap(), tg.ap()
        return bl, bg

    gu_l, gu_g, d_l, d_g = [], [], [], []
    for i in range(2):
        a, b = ag_pair("gu%d" % i, GU_W)
        gu_l.append(a)
        gu_g.append(b)
        a, b = ag_pair("dd%d" % i, D_W)
        d_l.append(a)
        d_g.append(b)
    io_l, io_g = ag_pair("io", IO_W)
    GU_SPLIT = 6
    gu0a_l, gu0a_g = ag_pair("gu0a", GU_SPLIT * 8192)
    gu0b_l, gu0b_g = ag_pair("gu0b", (22 - GU_SPLIT) * 8192)

    def gu_step(i, s_):
        if i == 0:
            if s_ < GU_SPLIT:
                return gu0a_g[:, s_ * 8192:(s_ + 1) * 8192]
            return gu0b_g[:, (s_ - GU_SPLIT) * 8192:(s_ - GU_SPLIT + 1) * 8192]
        return gu_g[i][:, s_ * 8192:(s_ + 1) * 8192]
    x1s = dram_tmp("x1s", [NT, 128, 4, D], F32)
    x1Ts = dram_tmp("x1Ts", [NT, 128, 16, TT], BF16)
    hsend = dram_tmp("hsend", [128, 36], F32)
    hrecv = dram_tmp("hrecv", [4 * 128, 36], F32)
    ssend = dram_tmp("ssend", [128, NST], F32)
    srecv = dram_tmp("srecv", [8 * 128, NST], F32)

    cur = [SB_LO]

    def sb(name, shape, dt, addr=None):
        esz = 2 if dt == BF16 else 4
        n = esz
        for s in shape[1:]:
            n *= s
        if addr is None:
            addr = cur[0]
            cur[0] += (n + 63) // 64 * 64
            assert cur[0] <= SB_HI, (name, cur[0])
        else:
            assert addr + n <= SB_HI, (name, addr, n)
        h = nc.alloc_sbuf_tensor_at(name, shape, dt, offset=addr)
        return Buf(h, "sb", addr, shape, esz)

    cst = sb("cst", [128, NCST], F32)

    def C(name, a=None, b=None):
        lo, hi = _off[name]
        if a is None:
            return cst[:, lo:hi]
        return cst[:, lo + a:lo + (b if b is not None else a + 1)]

    ident_bf = sb("ident_bf", [128, 128], BF16)
    dmat = sb("dmat", [128, 8, 128], BF16)
    a_bc = sb("a_bc", [128, 16], F32)
    S_f = sb("S_f", [128, 2048], F32)
    S_b = sb("S_b", [128, 2048], BF16)
    st_f = sb("st_f", [128, 1024], F32)
    st_b = sb("st_b", [128, 1024], BF16)
    halo = sb("halo", [128, 12, 3], F32)
    halo0 = sb("halo0", [128, 12, 3], F32)
    xl3 = sb("xl3", [128, 16, 3], BF16)
    small = sb("small", [128, 384], F32)
    atot = sb("atot", [128, 16], F32)
    NW = 3
    Wp = [sb("W%d" % i, [128, 8192], BF16) for i in range(NW)]
    xT = [sb("xT%d" % i, [128, 16, TT], BF16) for i in range(2)]
    gb = [[sb("gb%d%d" % (i, k), [128, D], F32, addr=xT[i].base + k * 8192) for k in range(2)] for i in range(2)]
    xtok = sb("xtok", [128, 4, D], F32)
    rbuf = sb("rbuf", [128, NST], F32, addr=xtok.base)
    rbuf2 = sb("rbuf2", [128, NST], F32, addr=xtok.base + 16384)
    hr = sb("hr", [128, 4, 36], F32, addr=xtok.base + 16384)
    arena0 = cur[0]
    ARENA = SB_HI - arena0
    hT = sb("hT", [128, NFC, TT], BF16, addr=arena0)
    sgt = [sb("sgt%d" % i, [128, TT], F32, addr=arena0 + 45056 + i * 2048) for i in range(2)]
    x1T_stage = sb("x1Tst", [128, 16, TT], BF16, addr=arena0 + 28 * 1024)
    lnst = sb("lnst", [128, 24], F32, addr=arena0 + 45056 + 4096)
    lnmv = sb("lnmv", [128, 4, 4], F32, addr=arena0 + 45056 + 4096 + 128)
    assert 45056 + 4096 + 256 <= ARENA, ARENA
    gbB = [sb("gbB%d" % k, [128, D], F32, addr=arena0 + 51200 + k * 8192) for k in range(2)]

    mcur = [arena0]

    def ma(name, shape, dt):
        esz = 2 if dt == BF16 else 4
        n = esz
        for s in shape[1:]:
            n *= s
        addr = mcur[0]
        mcur[0] += (n + 63) // 64 * 64
        assert mcur[0] <= SB_HI, (name, mcur[0] - arena0, ARENA)
        return sb(name, shape, dt, addr=addr)

    mixT = ma("mixT", [128, 16, TT], BF16)
    mbase = mcur[0]
    xbcT = ma("xbcT", [128, 12, TT], BF16)
    sz = ma("sz", [128, 4, 1024], BF16)
    pre = [ma("pre%d" % i, [128, TT + 3], F32) for i in range(2)]
    acc = [ma("acc%d" % i, [128, TT], F32) for i in range(2)]
    dtt = ma("dtt", [128, 4, 16], F32)
    dAt = ma("dAt", [128, 4, 16], F32)
    Rm = ma("Rm", [128, 16, 128], F32)
    seg = sb("seg", [128, 16, 128], F32, addr=Rm.base)
    Pm = ma("Pm", [128, 16, 128], BF16)
    xdt = ma("xdt", [128, 16, 64], BF16)
    xsw = ma("xsw", [128, 16, 64], BF16)
    bmt = ma("bmt", [128, 256], BF16)
    ysb = ma("ysb", [128, 1024], F32)
    mcur[0] = mbase
    qT = ma("qT", [128, 8, TT], BF16)
    kT = ma("kT", [128, 8, TT], BF16)
    vt = ma("vt", [128, 4, 1024], BF16)
    sgT = ma("sgT", [128, 8, TT], BF16)
    ncs = [ma("ncs%d" % i, [128, TT], F32) for i in range(2)]
    rt = [ma("rt%d" % i, [128, TT], F32) for i in range(2)]
    kd = ma("kd", [128, 1024], BF16)
    Pt = ma("Pt", [128, 512], BF16)
    oa = ma("oa", [128, 1024], F32)
    ob = ma("ob", [128, 1024], F32)
    tmpT = sb("tmpT", [128, 8, 128], F32, addr=oa.base)
    posi = sb("posi", [128, TT], I32, addr=rt[1].base)

    psb = [nc.alloc_psum_tensor("ps%d" % b, [128, 512], F32) for b in range(8)]

    def PS(b, lo=0, hi=512, shape=None):
        ap = psb[b][:, lo:hi]
        if shape is not None:
            names = " ".join("a%d" % i for i in range(len(shape)))
            kw = {"a%d" % i: s for i, s in enumerate(shape)}
            ap = ap.rearrange("p (%s) -> p %s" % (names, names), **kw)
        return V(ap, "ps", b * 2048 + lo * 4, b * 2048 + hi * 4)

    def PSbf(b, lo=0, hi=1024, shape=None):
        ap = psb[b].bitcast(BF16)[:, lo:hi]
        if shape is not None:
            names = " ".join("a%d" % i for i in range(len(shape)))
            kw = {"a%d" % i: s for i, s in enumerate(shape)}
            ap = ap.rearrange("p (%s) -> p %s" % (names, names), **kw)
        return V(ap, "ps", b * 2048 + lo * 2, b * 2048 + hi * 2)

    pscur = [0]

    def psalloc(n=1):
        if pscur[0] % n:
            pscur[0] += n - pscur[0] % n
        if pscur[0] + n > 8:
            pscur[0] = 0
        b = pscur[0]
        pscur[0] += n
        return b

    def rs(v, shape):
        names = " ".join("a%d" % i for i in range(len(shape)))
        kw = {"a%d" % i: s for i, s in enumerate(shape)}
        return V(v.ap.rearrange("p (%s) -> p %s" % (names, names), **kw), v.space, v.lo, v.hi)

    def fl(v):
        nd = len(v.ap.shape) - 1
        names = " ".join("a%d" % i for i in range(nd))
        return V(v.ap.rearrange("p %s -> p (%s)" % (names, names)), v.space, v.lo, v.hi)

    def bc(v, axis, n):
        ap = v.ap.unsqueeze(axis)
        shp = list(ap.shape)
        shp[axis] = n
        return V(ap.broadcast_to(shp), v.space, v.lo, v.hi)

    def dma(eng, out, in_, slot, batch=False):
        S.op(eng, lambda e: e.dma_start(out=out.ap, in_=in_.ap), reads=[in_], writes=[out], slot=slot, batch=batch)

    def mm_group(out, pairs, reads, start=True, stop=True):
        def fn(e):
            n = len(pairs)
            ins = None
            for i, (l, r) in enumerate(pairs):
                ins = e.matmul(out.ap, lhsT=l, rhs=r, start=(start and i == 0), stop=(stop and i == n - 1))
            return ins
        S.op("pe", fn, reads=reads, writes=[out])

    def transp(out, in_, ident):
        S.op("pe", lambda e: e.transpose(out.ap, in_.ap, ident.ap), reads=[in_, ident], writes=[out])

    def act(out, in_, func, bias=None, scale=None):
        kw = {}
        rd = [in_]
        for nm, val in (("bias", bias), ("scale", scale)):
            if val is None:
                continue
            if isinstance(val, V):
                kw[nm] = val.ap
                rd.append(val)
            else:
                kw[nm] = val
        S.op("act", lambda e: e.activation(out=out.ap, in_=in_.ap, func=func, **kw), reads=rd, writes=[out])

    def tt(out, a, b, op, eng="dve"):
        S.op(eng, lambda e: e.tensor_tensor(out=out.ap, in0=a.ap, in1=b.ap, op=op), reads=[a, b], writes=[out])

    def ts(out, a, s1, s2, op0, op1=None, eng="dve"):
        rd = [a]
        v1 = s1.ap if isinstance(s1, V) else s1
        v2 = s2.ap if isinstance(s2, V) else s2
        if isinstance(s1, V):
            rd.append(s1)
        if isinstance(s2, V):
            rd.append(s2)
        if op1 is None:
            S.op(eng, lambda e: e.tensor_scalar(out=out.ap, in0=a.ap, scalar1=v1, scalar2=None, op0=op0),
                 reads=rd, writes=[out])
        else:
            S.op(eng, lambda e: e.tensor_scalar(out=out.ap, in0=a.ap, scalar1=v1, scalar2=v2, op0=op0, op1=op1),
                 reads=rd, writes=[out])

    def stt(out, a, s, b, op0, op1, eng="dve"):
        rd = [a, b]
        sv = s.ap if isinstance(s, V) else s
        if isinstance(s, V):
            rd.append(s)
        S.op(eng, lambda e: e.scalar_tensor_tensor(out=out.ap, in0=a.ap, scalar=sv, in1=b.ap, op0=op0, op1=op1),
             reads=rd, writes=[out])

    def copy(out, in_, eng="dve"):
        if eng == "act":
            act(out, in_, AF.Copy)
        else:
            S.op(eng, lambda e: e.tensor_copy(out=out.ap, in_=in_.ap), reads=[in_], writes=[out])

    def memset(v, val):
        S.op("dve", lambda e: e.memset(v.ap, val), reads=[], writes=[v])

    def bnstats(o, v):
        S.op("dve", lambda e: e.bn_stats(out=o.ap, in_=v.ap), reads=[v], writes=[o])

    def bnaggr(o, v):
        S.op("dve", lambda e: e.bn_aggr(out=o.ap, in_=v.ap), reads=[v], writes=[o])

    def rstd_from_var(var_v, tmp_v, out_v, eps):
        act(tmp_v, var_v, AF.Sqrt, bias=eps)
        S.op("dve", lambda e: e.reciprocal(out=out_v.ap, in_=tmp_v.ap), reads=[tmp_v], writes=[out_v])

    def allgather(in_buf, out_buf, groups, slot):
        iv, ov = in_buf.whole(), out_buf.whole()
        if hasattr(in_buf, "raw"):
            iv = V(in_buf.raw, iv.space, iv.lo, iv.hi)
            ov = V(out_buf.raw, ov.space, ov.lo, ov.hi)
        S.op("pool", lambda e: e.collective_compute("AllGather", ALU.bypass, replica_groups=groups,
                                                    ins=[iv.ap], outs=[ov.ap]),
             reads=[iv], writes=[ov], slot=slot, cc=True)

    ALL8 = [list(range(8))]
    SEQ4 = [[0, 1, 2, 3], [4, 5, 6, 7]]

    dma("sp", cst.whole(), cst_d.whole(), slot="cst")
    copy(ident_bf.whole(), C("ident"))
    for cc in range(8):
        ts(dmat[:, cc, :], C("ident"), C("dcol", cc), None, ALU.mult)
    act(a_bc.whole(), C("alog"), AF.Exp)
    ts(a_bc.whole(), a_bc.whole(), -1.0, None, ALU.mult)

    cast_n = {}

    def cast(dst_v, src_ap, slot):
        k = cast_n.get(dst_v.space, 0)
        cast_n[dst_v.space] = k + 1
        tok = V(dst_v.ap, dst_v.space, k, k + 1)
        S.op("pool", lambda e: e.dma_start(out=tok.ap, in_=src_ap), reads=[], writes=[tok], slot=slot, batch=True)

    def cast_ffn(i):
        parts = [(gu_l[i], gu_g[i], 0, 22)] if i else [(gu0a_l, gu0a_g, 0, GU_SPLIT), (gu0b_l, gu0b_g, GU_SPLIT, 22)]
        for pi_, (pl, pg, s0, s1) in enumerate(parts):
            lv = pl.h.rearrange("pp (s k dc f) -> pp s k dc f", s=s1 - s0, k=2, dc=16)
            for s_ in range(s0, s1):
                for k, src in ((0, wg_d[i]), (1, wu_d[i])):
                    sv = src.h.rearrange("dc pp (s f) -> pp s dc f", f=256)
                    cast(V(lv[:, s_ - s0, k], pl.space, 0, pl.nbytes), sv[:, s_], ("cgu", i, pi_))
            allgather(pl, pg, ALL8, ("ag_gu", i, pi_))
        lv = d_l[i].h.rearrange("pp (dr g fc n) -> pp dr g fc n", dr=4, g=4, fc=11)
        sv = wd_d[i].h.rearrange("(g fc) pp (dr n) -> pp dr g fc n", fc=11, n=512)
        for dr in range(4):
            for g in range(4):
                cast(V(lv[:, dr, g], d_l[i].space, 0, d_l[i].nbytes), sv[:, dr, g], ("cd", i))
        allgather(d_l[i], d_g[i], ALL8, ("ag_d", i))

    def cast_io():
        lv = io_l.h[:, 0:IO_IN].rearrange("pp (g dc n) -> pp g dc n", g=13, dc=16)
        sv = win_d.h[:, :, 0:6656].rearrange("dc pp (g n) -> pp g dc n", n=512)
        for g in range(13):
            cast(V(lv[:, g], io_l.space, 0, io_l.nbytes), sv[:, g], "cio")
        lv = io_l.h[:, IO_DT:IO_DT + 256].rearrange("pp (dc n) -> pp dc n", dc=16)
        sv = win_d.h[:, :, 6656:6672].rearrange("dc pp n -> pp dc n")
        cast(V(lv, io_l.space, 0, io_l.nbytes), sv, "cio")
        lv = io_l.h[:, IO_OUT:IO_OUT + 4 * 8192].rearrange("pp (dr mc n) -> pp dr mc n", dr=4, mc=16)
        sv = wout_d.h.rearrange("mc pp (dr n) -> pp dr mc n", n=512)
        for dr in range(4):
            cast(V(lv[:, dr], io_l.space, 0, io_l.nbytes), sv[:, dr], "cio")
        allgather(io_l, io_g, ALL8, "ag_io")

    cast_ffn(0)
    if stage >= 2:
        cast_io()
    if stage >= 3:
        cast_ffn(1)

    wslot = [0]

    def load_w(src_view, shape):
        i = wslot[0] % NW
        wslot[0] += 1
        nel = 1
        for s in shape[1:]:
            nel *= s
        dst = V(Wp[i].h[:, 0:nel], "sb", Wp[i].base, Wp[i].base + nel * 2)
        dma("sp", dst, src_view, slot=("W", i))
        view = Wp[i].h[:, 0:nel]
        if len(shape) > 2:
            names = " ".join("a%d" % k for k in range(len(shape) - 1))
            kw = {"a%d" % k: s for k, s in enumerate(shape[1:])}
            view = view.rearrange("p (%s) -> p %s" % (names, names), **kw)
        return view, dst

    def io_grp(g):
        return io_g[:, g * 8192:(g + 1) * 8192]

    def transpose_tile(src_tok, dstT):
        for dc in range(16):
            b = psalloc()
            for c in range(4):
                transp(PS(b, c * 128, (c + 1) * 128), src_tok[:, c, dc * 128:(dc + 1) * 128], C("ident"))
            copy(dstT[:, dc, :], PS(b), eng=("act" if dc % 2 == 0 else "dve"))

    def layer_norm(xt_buf, gbuf, ln_idx, eps):
        dma("sp", gbuf[0].whole(), lngb_d[2 * ln_idx], slot=("gb", 0))
        dma("sp", gbuf[1].whole(), lngb_d[2 * ln_idx + 1], slot=("gb", 1))
        for c in range(4):
            for q in range(4):
                bnstats(lnst[:, q * 6:(q + 1) * 6], xt_buf[:, c, q * 512:(q + 1) * 512])
            bnaggr(lnmv[:, c, 0:2], lnst.whole())
        rstd_from_var(lnmv[:, :, 1:2], lnmv[:, :, 2:3], lnmv[:, :, 3:4], eps)
        for c in range(4):
            ts(xt_buf[:, c, :], xt_buf[:, c, :], lnmv[:, c, 0:1], lnmv[:, c, 3:4], ALU.subtract, ALU.mult)
            tt(xt_buf[:, c, :], xt_buf[:, c, :], gbuf[0].whole(), ALU.mult)
            tt(xt_buf[:, c, :], xt_buf[:, c, :], gbuf[1].whole(), ALU.add)

    def ffn_tile(i, xTb, xt_buf, res_scale):
        for s in range(22):
            wv, wdst = load_w(gu_step(i, s), [128, 2, 16, 256])
            base = (s % 2) * 4
            for j in range(2):
                fc = 2 * s + j
                for k in range(2):
                    mm_group(PS(base + 2 * k + j),
                             [(wv[:, k, dc, j * 128:(j + 1) * 128], xTb[:, dc, :].ap) for dc in range(16)],
                             reads=[wdst, xTb.whole()])
                sg = sgt[fc % 2]
                act(sg.whole(), PS(base + j), AF.Silu)
                tt(hT[:, fc, :], sg.whole(), PS(base + 2 + j), ALU.mult)
        for dr in range(4):
            base = (dr % 2) * 4
            for g in range(4):
                o = (dr * 4 + g) * 5632
                wv, wdst = load_w(d_g[i][:, o:o + 5632], [128, 11, 512])
                for tcn in range(4):
                    mm_group(PS(base + tcn),
                             [(hT[:, g * 11 + f, tcn * 128:(tcn + 1) * 128].ap, wv[:, f, :]) for f in range(11)],
                             reads=[wdst, hT[:, g * 11:(g + 1) * 11, :]], start=(g == 0), stop=(g == 3))
            for tcn in range(4):
                v = xt_buf[:, tcn, dr * 512:(dr + 1) * 512]
                stt(v, PS(base + tcn), res_scale, v, ALU.mult, ALU.add)

    acum, tot, fs, te, et = (small[:, 0:16], small[:, 16:32], small[:, 32:48], small[:, 48:64], small[:, 64:80])
    gst = small[:, 96:120]
    gmv = sb("gmv", [128, 4, 4], F32, addr=small.base + 128 * 4)
    rst = small[:, 160:166]
    rmv = small[:, 168:176]
    tw = small[:, 176:192]
    hs = small[:, 192:228]

    def mixer_tile(t, xb, full):
        xw = xb.whole()
        wv, wd = load_w(io_g[:, IO_DT:IO_DT + 256], [128, 16, 16])
        b = psalloc()
        for c in range(4):
            mm_group(PS(b, c * 16, (c + 1) * 16),
                     [(xb[:, dc, c * 128:(c + 1) * 128].ap, wv[:, dc, :]) for dc in range(16)], reads=[wd, xw])
        tt(dtt.whole(), PS(b, 0, 64, shape=[4, 16]), bc(C("dtb"), 1, 4), ALU.add)
        act(dtt.whole(), dtt.whole(), AF.Exp)
        act(dtt.whole(), dtt.whole(), AF.Ln, bias=1.0)
        tt(dAt.whole(), dtt.whole(), bc(a_bc.whole(), 1, 4), ALU.mult)
        if full:
            for half in range(2):
                wv, wd = load_w(io_grp(8 + half), [128, 16, 512])
                for c in range(4):
                    b = psalloc()
                    mm_group(PS(b), [(xb[:, dc, c * 128:(c + 1) * 128].ap, wv[:, dc, :]) for dc in range(16)],
                             reads=[wd, xw])
                    act(sz[:, c, half * 512:(half + 1) * 512], PS(b), AF.Silu)
        for gi in range(3):
            wv, wd = load_w(io_grp(10 + gi), [128, 16, 512])
            for j in range(4):
                cc = gi * 4 + j
                b = psalloc()
                mm_group(PS(b), [(wv[:, dc, j * 128:(j + 1) * 128], xb[:, dc, :].ap) for dc in range(16)],
                         reads=[wd, xw])
                p, a = pre[cc % 2], acc[cc % 2]
                copy(p[:, 0:3], halo[:, cc, :], eng="act")
                copy(p[:, 3:TT + 3], PS(b), eng="act")
                ts(a.whole(), p[:, 0:TT], C("convw", cc * 4), None, ALU.mult)
                for k in range(1, 4):
                    stt(a.whole(), p[:, k:k + TT], C("convw", cc * 4 + k), a.whole(), ALU.mult, ALU.add)
                copy(halo[:, cc, :], p[:, TT:TT + 3], eng="act")
                act(xbcT[:, cc, :], a.whole(), AF.Silu, bias=C("convb", cc))
        for c in range(4):
            c0, c1_ = c * 128, (c + 1) * 128
            dAc = dAt[:, c, :]
            b = psalloc()
            mm_group(PS(b, 0, 16), [(C("U").ap, dAc.ap)], reads=[C("U"), dAc])
            mm_group(PS(b, 16, 32), [(C("ones").ap, dAc.ap)], reads=[C("ones"), dAc])
            copy(small[:, 0:32], PS(b, 0, 32), eng="act")
            act(fs, acum, AF.Exp)
            tt(te, tot, acum, ALU.subtract)
            act(te, te, AF.Exp)
            act(et, tot, AF.Exp)
            if not full:
                tt(atot.whole(), atot.whole(), tot, ALU.add)
            b = psalloc()
            for cc in range(8):
                transp(PSbf(b, cc * 128, (cc + 1) * 128), xbcT[:, cc, c0:c1_], ident_bf.whole())
            tt(xdt.whole(), PSbf(b, shape=[16, 64]), bc(dtt[:, c, :], 2, 64), ALU.mult)
            tt(xsw.whole(), xdt.whole(), bc(te, 2, 64), ALU.mult)
            b = psalloc()
            for g in range(2):
                transp(PSbf(b, g * 128, (g + 1) * 128), xbcT[:, 8 + g, c0:c1_], ident_bf.whole())
            copy(bmt.whole(), PSbf(b, 0, 256), eng="act")
            if full:
                tt(Rm.whole(), bc(C("U"), 1, 16), bc(dAc, 2, 128), ALU.mult)
                b4 = psalloc(4)
                for q in range(4):
                    rq = fl(Rm[:, q * 4:(q + 1) * 4, :])
                    mm_group(PS(b4 + q), [(C("ones").ap, rq.ap)], reads=[C("ones"), rq])
                for q in range(4):
                    tt(seg[:, q * 4:(q + 1) * 4, :], PS(b4 + q, shape=[4, 128]),
                       bc(small[:, q * 4:(q + 1) * 4], 2, 128), ALU.subtract)
                tt(seg.whole(), seg.whole(), bc(C("cmask"), 1, 16), ALU.add)
                act(seg.whole(), seg.whole(), AF.Exp)
                b = psalloc()
                for g in range(2):
                    bm, cm = xbcT[:, 8 + g, c0:c1_], xbcT[:, 10 + g, c0:c1_]
                    mm_group(PS(b, g * 128, (g + 1) * 128), [(bm.ap, cm.ap)], reads=[bm, cm])
                for g in range(2):
                    tt(Pm[:, g * 8:(g + 1) * 8, :], seg[:, g * 8:(g + 1) * 8, :],
                       bc(PS(b, g * 128, (g + 1) * 128), 1, 8), ALU.mult)
                by = psalloc(2)
                for h in range(16):
                    cc, half = h // 2, h % 2
                    col = (h % 8) * 64
                    xc = xbcT[:, cc, c0:c1_]
                    dm = dmat[:, cc, half * 64:(half + 1) * 64]
                    mm_group(PS(by + h // 8, col, col + 64), [(xc.ap, dm.ap), (Pm[:, h, :].ap, xdt[:, h, :].ap)],
                             reads=[xc, dm, Pm[:, h, :], xdt[:, h, :]])
                bi = psalloc(2)
                for g in range(2):
                    cm, sg_ = xbcT[:, 10 + g, c0:c1_], st_b[:, g * 512:(g + 1) * 512]
                    mm_group(PS(bi + g), [(cm.ap, sg_.ap)], reads=[cm, sg_])
                for g in range(2):
                    yv = ysb[:, g * 512:(g + 1) * 512]
                    tt(rs(yv, [8, 64]), PS(bi + g, shape=[8, 64]), bc(small[:, 32 + g * 8:32 + (g + 1) * 8], 2, 64),
                       ALU.mult)
                    tt(yv, yv, PS(by + g), ALU.add)
                tt(ysb.whole(), ysb.whole(), sz[:, c, :], ALU.mult)
                for g in range(2):
                    yv = ysb[:, g * 512:(g + 1) * 512]
                    bnstats(rst, yv)
                    bnaggr(rmv[:, 0:2] if False else small[:, 168:170], rst)
                    stt(small[:, 170:171], small[:, 168:169], small[:, 168:169], small[:, 169:170], ALU.mult, ALU.add)
                    rstd_from_var(small[:, 170:171], small[:, 171:172], small[:, 172 + g:173 + g], EPS)
                    ts(yv, yv, small[:, 172 + g:173 + g], None, ALU.mult)
                for half in range(2):
                    b = psalloc()
                    for j in range(4):
                        cc = half * 4 + j
                        transp(PS(b, j * 128, (j + 1) * 128), ysb[:, cc * 128:(cc + 1) * 128], C("ident"))
                    for j in range(4):
                        cc = half * 4 + j
                        act(mixT[:, 8 + cc, c0:c1_], PS(b, j * 128, (j + 1) * 128), AF.Identity, scale=C("ng", cc))
            bs = psalloc(2)
            for g in range(2):
                bmv, xv = bmt[:, g * 128:(g + 1) * 128], fl(xsw[:, g * 8:(g + 1) * 8, :])
                mm_group(PS(bs + g), [(bmv.ap, xv.ap)], reads=[bmv, xv])
            for g in range(2):
                sv = st_f[:, g * 512:(g + 1) * 512]
                tt(rs(sv, [8, 64]), rs(sv, [8, 64]), bc(small[:, 64 + g * 8:64 + (g + 1) * 8], 2, 64), ALU.mult)
                tt(sv, sv, PS(bs + g), ALU.add)
            copy(st_b.whole(), st_f.whole(), eng="act")

        dma("sp", posi.whole(), pos_d[:, t * TT:(t + 1) * TT], slot="posi")
        th, kf = rt[0].whole(), ncs[0].whole()
        copy(th, posi.whole())
        ts(th, th, C("invf"), None, ALU.mult)
        ts(kf, th, 1.0 / TWO_PI, None, ALU.mult)
        copy(posi.whole(), kf)
        copy(kf, posi.whole())
        stt(th, kf, -CW1, th, ALU.mult, ALU.add)
        stt(th, kf, -CW2, th, ALU.mult, ALU.add)
        ts(th, th, 3.141592, -3.141592, ALU.min, ALU.max)
        sinv, cosv = ncs[1].whole(), ncs[0].whole()
        act(sinv, th, AF.Sin)
        ts(rt[1].whole(), th, -1.0, None, ALU.mult)
        tt(th, th, rt[1].whole(), ALU.max)
        act(cosv, th, AF.Sin, scale=-1.0, bias=math.pi / 2.0)
        for (isq, dst, grp0) in ((True, qT, 0), (False, kT, 2)):
            if isq and not full:
                continue
            for gi in range(2):
                wv, wd = load_w(io_grp(grp0 + gi), [128, 16, 512])
                bb = psalloc(4)
                for j in range(4):
                    mm_group(PS(bb + j), [(wv[:, dc, j * 128:(j + 1) * 128], xb[:, dc, :].ap) for dc in range(16)],
                             reads=[wd, xw])
                for hh in range(2):
                    pe_, po_ = PS(bb + 2 * hh), PS(bb + 2 * hh + 1)
                    ce = gi * 4 + 2 * hh
                    r0, r1 = rt[0].whole(), rt[1].whole()
                    tt(r0, pe_, cosv, ALU.mult)
                    tt(r1, po_, sinv, ALU.mult)
                    tt(dst[:, ce, :], r0, r1, ALU.subtract)
                    tt(r0, po_, cosv, ALU.mult)
                    tt(r1, pe_, sinv, ALU.mult)
                    tt(dst[:, ce + 1, :], r0, r1, ALU.add)
        for half in range(2):
            wv, wd = load_w(io_grp(4 + half), [128, 16, 512])
            for c in range(4):
                b = psalloc()
                mm_group(PS(b), [(xb[:, dc, c * 128:(c + 1) * 128].ap, wv[:, dc, :]) for dc in range(16)],
                         reads=[wd, xw])
                copy(vt[:, c, half * 512:(half + 1) * 512], PS(b), eng="act")
        if full:
            for gi in range(2):
                wv, wd = load_w(io_grp(6 + gi), [128, 16, 512])
                for j in range(4):
                    b = psalloc()
                    mm_group(PS(b), [(wv[:, dc, j * 128:(j + 1) * 128], xb[:, dc, :].ap) for dc in range(16)],
                             reads=[wd, xw])
                    act(sgT[:, gi * 4 + j, :], PS(b), AF.Silu)
        for c in range(4):
            c0, c1_ = c * 128, (c + 1) * 128
            b = psalloc()
            for kc in range(8):
                transp(PSbf(b, kc * 128, (kc + 1) * 128), kT[:, kc, c0:c1_], ident_bf.whole())
            for h in range(4):
                ts(kd[:, h * 256:(h + 1) * 256], PSbf(b, h * 256, (h + 1) * 256), C("kdec", h), None, ALU.mult)
            if full:
                b = psalloc()
                for h in range(4):
                    kk, qq = kT[:, 2 * h:2 * h + 2, c0:c1_], qT[:, 2 * h:2 * h + 2, c0:c1_]
                    mm_group(PS(b, h * 128, (h + 1) * 128),
                             [(kT[:, 2 * h + e, c0:c1_].ap, qT[:, 2 * h + e, c0:c1_].ap) for e in range(2)],
                             reads=[kk, qq])
                tt(Pt.whole(), PS(b), C("dmask"), ALU.mult)
                ba = psalloc(2)
                bo = psalloc(2)
                for h in range(4):
                    col = (h % 2) * 256
                    pv, vv = Pt[:, h * 128:(h + 1) * 128], vt[:, c, h * 256:(h + 1) * 256]
                    mm_group(PS(ba + h // 2, col, col + 256), [(pv.ap, vv.ap)], reads=[pv, vv])
                    qq, ss = qT[:, 2 * h:2 * h + 2, c0:c1_], S_b[:, 2 * h * 256:(2 * h + 2) * 256]
                    mm_group(PS(bo + h // 2, col, col + 256),
                             [(qT[:, 2 * h + e, c0:c1_].ap, S_b[:, (2 * h + e) * 256:(2 * h + e + 1) * 256].ap)
                              for e in range(2)], reads=[qq, ss])
                for g2 in range(2):
                    copy(oa[:, g2 * 512:(g2 + 1) * 512], PS(ba + g2), eng="act")
                for h in range(4):
                    col = (h % 2) * 256
                    stt(ob[:, h * 256:(h + 1) * 256], PS(bo + h // 2, col, col + 256), C("qdec", h),
                        oa[:, h * 256:(h + 1) * 256], ALU.mult, ALU.add)
                for h in range(4):
                    bnstats(small[:, 96 + h * 6:102 + h * 6], ob[:, h * 256:(h + 1) * 256])
                    bnaggr(gmv[:, h, 0:2], small[:, 96 + h * 6:102 + h * 6])
                rstd_from_var(gmv[:, :, 1:2], gmv[:, :, 2:3], gmv[:, :, 3:4], EPS)
                for h in range(4):
                    ov = ob[:, h * 256:(h + 1) * 256]
                    ts(ov, ov, gmv[:, h, 0:1], gmv[:, h, 3:4], ALU.subtract, ALU.mult)
                for half in range(2):
                    b = psalloc()
                    for j in range(4):
                        mc = half * 4 + j
                        transp(PS(b, j * 128, (j + 1) * 128), ob[:, mc * 128:(mc + 1) * 128], C("ident"))
                    for j in range(4):
                        mc = half * 4 + j
                        act(tmpT[:, mc, :], PS(b, j * 128, (j + 1) * 128), AF.Identity, scale=C("gng", mc),
                            bias=C("gnb", mc))
                tt(mixT[:, 0:8, c0:c1_], tmpT.whole(), sgT[:, :, c0:c1_], ALU.mult)
            bs = psalloc(4)
            for h in range(4):
                for e in range(2):
                    idx = 2 * h + e
                    col = (idx % 2) * 256
                    kv, vv = kd[:, idx * 128:(idx + 1) * 128], vt[:, c, h * 256:(h + 1) * 256]
                    mm_group(PS(bs + idx // 2, col, col + 256), [(kv.ap, vv.ap)], reads=[kv, vv])
            for h in range(4):
                sv = S_f[:, h * 512:(h + 1) * 512]
                stt(sv, sv, GAM[h] ** 128.0, PS(bs + h), ALU.mult, ALU.add)
            copy(S_b.whole(), S_f.whole(), eng="act")

    c1 = 0.5 / ALPHA
    eps1 = EPS / (ALPHA * ALPHA)
    for n, t in enumerate(tiles_a1):
        xb = xT[n % 2]
        dma("sp", xtok.whole(), V(x_d.h[t * TT:(t + 1) * TT, :].rearrange("(c p) d -> p c d", p=128), "x", t, t + 1),
            slot="xtok")
        transpose_tile(xtok, xb)
        if DBG != 1:
            ffn_tile(0, xb, xtok, c1)
        if DBG not in (1, 2):
            layer_norm(xtok, gb[n % 2], 0, eps1)
        if stage == 1:
            dma("sp", V(out_d.h[t * TT:(t + 1) * TT, :].rearrange("(c p) d -> p c d", p=128), "out", t, t + 1),
                xtok.whole(), slot="xtok_st")
            continue
        dma("sp", x1s[t], xtok.whole(), slot="xtok_st")
        transpose_tile(xtok, x1T_stage)
        dma("sp", x1Ts[t], x1T_stage.whole(), slot="x1T_st")
        if t == 3:
            copy(xl3.whole(), x1T_stage[:, :, TT - 3:TT])
        if n == 1 and DBG != 5:
            b = psalloc()
            for gi in range(3):
                wv, wd = load_w(io_grp(10 + gi), [128, 16, 512])
                for j in range(4):
                    cc = gi * 4 + j
                    mm_group(PS(b, cc * 3, cc * 3 + 3),
                             [(wv[:, dc, j * 128:(j + 1) * 128], xl3[:, dc, :].ap) for dc in range(16)],
                             reads=[wd, xl3.whole()])
            copy(hs, PS(b, 0, 36))
            dma("sp", hsend.whole(), hs, slot="hs")
            allgather(hsend, hrecv, SEQ4, "ag_h")

    if stage >= 2 and DBG != 5:
        dma("sp", hr.whole(), V(hrecv.h.rearrange("(r p) n -> p r n", p=128), hrecv.space, 0, hrecv.nbytes), slot="hr")
        h0 = fl(halo0.whole())
        ts(h0, hr[:, 0, :], C("ohalo", 0), None, ALU.mult)
        for r in range(1, 4):
            stt(h0, hr[:, r, :], C("ohalo", r), h0, ALU.mult, ALU.add)
        for v in (S_f, S_b, st_f, st_b, atot):
            memset(v.whole(), 0.0)
        copy(halo.whole(), halo0.whole())
        for t in range(NT if DBG != 3 else 0):
            xb = xT[t % 2]
            dma("sp", xb.whole(), x1Ts[t], slot=("xT", t % 2))
            mixer_tile(t, xb, False)
    if stage >= 2 and DBG not in (3, 4, 5):
        dma("sp", ssend[:, 0:2048], S_f.whole(), slot="ss0")
        dma("sp", ssend[:, 2048:3072], st_f.whole(), slot="ss1")
        dma("sp", ssend[:, 3072:3088], atot.whole(), slot="ss2")
        allgather(ssend, srecv, ALL8, "ag_s")
        for v in (S_f, st_f):
            memset(v.whole(), 0.0)
        for i_, r in enumerate((0, 1, 2, 4, 5, 6)):
            rb_ = (rbuf, rbuf2)[i_ % 2]
            dma("sp", rb_.whole(), srecv[r * 128:(r + 1) * 128, :], slot=("rbuf", i_ % 2))
            ts(rb_[:, 0:3072], rb_[:, 0:3072], C("mr", r), None, ALU.mult)
            for h in range(4):
                sv = S_f[:, h * 512:(h + 1) * 512]
                stt(sv, sv, C("retw", r * 4 + h), rb_[:, h * 512:(h + 1) * 512], ALU.mult, ALU.add)
            ts(tw, rb_[:, 3072:3088], C("mr", r), None, ALU.mult)
            act(tw, tw, AF.Exp)
            for g in range(2):
                sv = st_f[:, g * 512:(g + 1) * 512]
                tt(rs(sv, [8, 64]), rs(sv, [8, 64]), bc(small[:, 176 + g * 8:176 + (g + 1) * 8], 2, 64), ALU.mult)
                tt(sv, sv, rb_[:, 2048 + g * 512:2048 + (g + 1) * 512], ALU.add)
        copy(S_b.whole(), S_f.whole(), eng="act")
        copy(st_b.whole(), st_f.whole(), eng="act")
        copy(halo.whole(), halo0.whole())

    if stage >= 3:
        for t in range(NT):
            xb, xo = xT[0], xT[1]
            if t == 0:
                dma("sp", xb.whole(), x1Ts[t], slot=("xT", 0))
            mixer_tile(t, xb, True)
            if t + 1 < NT:
                dma("sp", xb.whole(), x1Ts[t + 1], slot=("xT", 0))
            dma("sp", xtok.whole(), x1s[t], slot="xtok")
            for dr in range(4):
                wv, wd = load_w(io_g[:, IO_OUT + dr * 8192:IO_OUT + (dr + 1) * 8192], [128, 16, 512])
                for tcn in range(4):
                    b = psalloc()
                    mm_group(PS(b), [(mixT[:, mc, tcn * 128:(tcn + 1) * 128].ap, wv[:, mc, :]) for mc in range(16)],
                             reads=[wd, mixT.whole()])
                    v = xtok[:, tcn, dr * 512:(dr + 1) * 512]
                    stt(v, PS(b), 1.0 / ALPHA, v, ALU.mult, ALU.add)
            layer_norm(xtok, gb[1], 1, eps1)
            transpose_tile(xtok, xo)
            ffn_tile(1, xo, xtok, c1)
            layer_norm(xtok, gbB, 2, eps1)
            dma("pool", V(out_d.h[t * TT:(t + 1) * TT, :].rearrange("(c p) d -> p c d", p=128), "out", t, t + 1),
                xtok.whole(), slot="out_st")
    elif stage == 2:
        dma("sp", V(out_d.h[0:128, :], "out", 0, 1), S_f.whole(), slot="dbg0")
        dma("sp", V(out_d.h[128:256, 0:1024], "out", 1, 2), st_f.whole(), slot="dbg1")

    S.emit(nc)
    es.close()
    return nc


def _perm_cols():
    idx = np.arange(DIN)
    for base in (0, 1024):
        for h in range(4):
            o = base + h * 256
            idx[o:o + 128] = o + np.arange(0, 256, 2)
            idx[o + 128:o + 256] = o + np.arange(1, 256, 2)
    return idx


def make_in_maps(x, positions, ffn1_w_gate, ffn1_w_up, ffn1_w_down, ln1_gain, ln1_bias, mix_w_in, ret_gn_gain,
                 ret_gn_bias, ssd_conv_w, ssd_conv_b, ssd_dt_bias, ssd_a_log, ssd_d, ssd_norm_gain, mix_w_out,
                 ln2_gain, ln2_bias, ffn2_w_gate, ffn2_w_up, ffn2_w_down, ln3_gain, ln3_bias):
    A = lambda a: np.asarray(a, np.float32)
    win = A(mix_w_in)[0][:, _perm_cols()]
    lngb = np.ascontiguousarray(np.stack([np.broadcast_to(A(v).reshape(1, D), (128, D)) for v in
                                          (ln1_gain, ln1_bias, ln2_gain, ln2_bias, ln3_gain, ln3_bias)]))
    full = {"wg1": A(ffn1_w_gate)[0], "wu1": A(ffn1_w_up)[0], "wd1": A(ffn1_w_down)[0],
            "wg2": A(ffn2_w_gate)[0], "wu2": A(ffn2_w_up)[0], "wd2": A(ffn2_w_down)[0],
            "win": win, "wout": A(mix_w_out)[0]}
    xs = A(x)
    ps = np.asarray(positions, np.int32)
    maps = []
    for c in range(8):
        b, r = c // 4, c % 4
        m = {"lngb": lngb}
        for k, w in full.items():
            rows, cols = w.shape
            m[k] = np.ascontiguousarray(w.reshape(rows // 128, 128, cols)[:, 16 * c:16 * (c + 1), :])
        m["x"] = np.ascontiguousarray(xs[b, r * TOK:(r + 1) * TOK, :])
        m["pos"] = np.ascontiguousarray(np.broadcast_to(ps[b, r * TOK:(r + 1) * TOK][None, :], (128, TOK)))
        m["cst"] = _host_consts(c, A(ssd_conv_w)[0], A(ssd_conv_b)[0], A(ret_gn_gain)[0], A(ret_gn_bias)[0],
                                A(ssd_norm_gain)[0], A(ssd_d)[0], A(ssd_dt_bias)[0], A(ssd_a_log)[0])
        maps.append(m)
    return maps


_NC_CACHE = {}


def kernel(**inputs):
    stage = 3
    if stage not in _NC_CACHE:
        _NC_CACHE[stage] = build(stage)
    nc = _NC_CACHE[stage]
    maps = make_in_maps(**inputs)
    res = run_bass_kernel_spmd(nc, maps, core_ids=list(range(8)), trace=True)
    out = np.empty((2, 8192, D), np.float32)
    for c in range(8):
        b, r = c // 4, c % 4
        out[b, r * TOK:(r + 1) * TOK, :] = np.asarray(res.results[c]["out"], np.float32)
    return out
```

```python
import math
import os
DBG = int(os.environ.get('KDBG', '0'))
import contextlib
import numpy as np
import concourse.bass as bass
import concourse.mybir as mybir
from concourse.bass_utils import run_bass_kernel_spmd

F32 = mybir.dt.float32
BF16 = mybir.dt.bfloat16
I32 = mybir.dt.int32
AF = mybir.ActivationFunctionType
ALU = mybir.AluOpType

D = 2048
DFF = 5632
NFC = 44
TOK = 2048
TT = 512
NT = TOK // TT
CH = 128
DIN = 6672
ALPHA = 2.0 ** 0.25
EPS = 1e-5
GAM = [1.0 - 2.0 ** (-5.0 - h) for h in range(4)]

SB_LO = 16384 + 1024
SB_HI = 229376 - 256


class V:
    __slots__ = ("ap", "space", "lo", "hi")

    def __init__(self, ap, space, lo, hi):
        self.ap, self.space, self.lo, self.hi = ap, space, lo, hi


class Buf:
    def __init__(self, h, space, base, shape, esz, pdim=True):
        self.h, self.space, self.base, self.shape, self.esz, self.pdim = h, space, base, list(shape), esz, pdim
        st = [1] * len(shape)
        for i in range(len(shape) - 2, -1, -1):
            st[i] = st[i + 1] * shape[i + 1]
        self.st = st
        self.nbytes = (st[1] * shape[1] if pdim else st[0] * shape[0]) * esz

    def __getitem__(self, idx):
        if not isinstance(idx, tuple):
            idx = (idx,)
        idx = tuple(idx) + (slice(None),) * (len(self.shape) - len(idx))
        lo = 0
        hi = 0
        for d, (i, n, s) in enumerate(zip(idx, self.shape, self.st)):
            if self.pdim and d == 0:
                continue
            if isinstance(i, slice):
                a = 0 if i.start is None else i.start
                b = n if i.stop is None else i.stop
            else:
                a, b = i, i + 1
            lo += a * s
            hi += (b - 1) * s
        return V(self.h[idx], self.space, self.base + lo * self.esz, self.base + (hi + 1) * self.esz)

    def whole(self):
        return self[tuple(slice(None) for _ in self.shape)]

    def cust(self, ap, lo_el, hi_el):
        return V(ap, self.space, self.base + lo_el * self.esz, self.base + hi_el * self.esz)


class Sched:
    ENG = ("pe", "act", "dve", "pool", "sp")

    def __init__(self):
        self.ops = {e: [] for e in self.ENG}
        self.Wr = {}
        self.Rd = {}
        self.slot_count = {}
        self.slot_batch = {}

    @staticmethod
    def _rng(v):
        if v.space == "ps":
            b = v.lo // 2048
            return b * 2048, (b + 1) * 2048
        return v.lo, v.hi

    def op(self, eng, fn, reads=(), writes=(), slot=None, batch=False, cc=False):
        deps = {}

        def add(ref):
            k = ref[:2]
            if deps.get(k, -1) < ref[2]:
                deps[k] = ref[2]

        for v in reads:
            vlo, vhi = self._rng(v)
            for (lo, hi, ref) in self.Wr.get(v.space, ()):
                if lo < vhi and vlo < hi:
                    add(ref)
            if v.space == "ps":
                for (lo, hi, k), ref in self.Rd.get(v.space, {}).items():
                    if lo < vhi and vlo < hi and k != ("c", eng):
                        add(ref)
        for v in writes:
            vlo, vhi = self._rng(v)
            for (lo, hi, ref) in self.Wr.get(v.space, ()):
                if lo < vhi and vlo < hi:
                    add(ref)
            for (lo, hi, _k), ref in self.Rd.get(v.space, {}).items():
                if lo < vhi and vlo < hi:
                    add(ref)
        idx = len(self.ops[eng])
        if slot is not None:
            c = self.slot_count.get(slot, 0) + 1
            self.slot_count[slot] = c
            self.slot_batch[slot] = (batch, cc)
            ref = ("d", slot, c)
        else:
            ref = ("c", eng, idx)
        self.ops[eng].append(dict(fn=fn, deps=deps, slot=slot, ref=ref, needed=False))
        for v in reads:
            vlo, vhi = self._rng(v)
            self.Rd.setdefault(v.space, {})[(vlo, vhi, ref[:2])] = ref
        for v in writes:
            vlo, vhi = self._rng(v)
            rd = self.Rd.get(v.space)
            if rd:
                for k in [k for k in rd if vlo <= k[0] and k[1] <= vhi]:
                    del rd[k]
            wl = self.Wr.setdefault(v.space, [])
            wl[:] = [w for w in wl if not (vlo <= w[0] and w[1] <= vhi)]
            wl.append((vlo, vhi, ref))

    def check_deadlock(self, same_eng_sync=True):
        sem = {}
        pc = {e: 0 for e in self.ENG}
        progress = True
        while progress:
            progress = False
            for eng in self.ENG:
                ops = self.ops[eng]
                while pc[eng] < len(ops):
                    o = ops[pc[eng]]
                    ok = True
                    for (kind, name), val in o["deps"].items():
                        if kind == "c":
                            if name == eng and (eng == "pe" or not same_eng_sync):
                                continue
                            key, target = ("c", name), self.ops[name][val]["val"]
                        else:
                            batch, cc = self.slot_batch[name]
                            n = self.slot_count[name] if batch else val
                            key, target = ("d", name), n
                        if sem.get(key, 0) < target:
                            ok = False
                            break
                    if not ok:
                        break
                    if o["slot"] is not None:
                        sem[("d", o["slot"])] = sem.get(("d", o["slot"]), 0) + 1
                    elif o["needed"]:
                        sem[("c", eng)] = sem.get(("c", eng), 0) + 1
                    pc[eng] += 1
                    progress = True
        stuck = {e: (pc[e], len(self.ops[e])) for e in self.ENG if pc[e] < len(self.ops[e])}
        if stuck:
            for e, (p, n) in stuck.items():
                print("DEADLOCK", e, p, n, self.ops[e][p]["deps"], self.ops[e][p]["ref"])
            raise RuntimeError("semaphore program deadlocks: %s" % stuck)
        print("deadlock check ok:", {e: len(self.ops[e]) for e in self.ENG}, "max sem", max(sem.values()))

    def emit(self, nc, same_eng_sync=True):
        for eng, ops in self.ops.items():
            for o in ops:
                for (kind, name), val in o["deps"].items():
                    if kind == "c":
                        if name == eng and (eng == "pe" or not same_eng_sync):
                            continue
                        self.ops[name][val]["needed"] = True
        for eng, ops in self.ops.items():
            c = 0
            for o in ops:
                if o["needed"]:
                    c += 1
                o["val"] = c
        self.check_deadlock(same_eng_sync)
        with contextlib.ExitStack() as es:
            esem = {e: es.enter_context(nc.semaphore("se_" + e)) for e in ("pe", "act", "dve", "pool")}
            ssem = {s: es.enter_context(nc.semaphore("sd_%d" % i)) for i, s in enumerate(self.slot_count)}
            block = es.enter_context(nc.Block())
            sched = self

            def run(eng, e):
                seen = {}
                for o in sched.ops[eng]:
                    for (kind, name), val in o["deps"].items():
                        if kind == "c":
                            if name == eng and (eng == "pe" or not same_eng_sync):
                                continue
                            sem, target = esem[name], sched.ops[name][val]["val"]
                        else:
                            batch, cc = sched.slot_batch[name]
                            n = sched.slot_count[name] if batch else val
                            sem, target = ssem[name], (n if cc else 16 * n)
                        if seen.get((kind, name), 0) >= target:
                            continue
                        seen[(kind, name)] = target
                        e.wait_ge(sem, target)
                    ins = o["fn"](e)
                    if o["slot"] is not None:
                        if sched.slot_batch[o["slot"]][1]:
                            ins.then_inc(ssem[o["slot"]])
                        else:
                            ins.then_inc(ssem[o["slot"]], 16)
                    elif o["needed"]:
                        ins.then_inc(esem[eng], 1)
                for s, n in sched.slot_count.items():
                    if sched.slot_owner.get(s) == eng:
                        cc = sched.slot_batch[s][1]
                        e.wait_ge(ssem[s], n if cc else 16 * n)

            self.slot_owner = {}
            for eng, ops in self.ops.items():
                for o in ops:
                    if o["slot"] is not None:
                        self.slot_owner[o["slot"]] = eng

            @block.tensor
            def _(e):
                run("pe", e)

            @block.scalar
            def _(e):
                run("act", e)

            @block.vector
            def _(e):
                run("dve", e)

            @block.gpsimd
            def _(e):
                run("pool", e)

            @block.sync
            def _(e):
                run("sp", e)


_off = {}
_n = 0
for _name, _w in (("ident", 128), ("U", 128), ("ones", 128), ("cmask", 128), ("dmask", 512), ("qdec", 4),
                  ("kdec", 4), ("invf", 1), ("convw", 48), ("convb", 12), ("gng", 8), ("gnb", 8), ("ng", 8),
                  ("dcol", 8), ("dtb", 16), ("alog", 16), ("ohalo", 4), ("mr", 7), ("retw", 28)):
    _off[_name] = (_n, _n + _w)
    _n += _w
NCST = _n


def _host_consts(core, ssd_conv_w, ssd_conv_b, ret_gn_gain, ret_gn_bias, ssd_norm_gain, ssd_d, ssd_dt_bias, ssd_a_log):
    c = np.zeros((128, NCST), np.float32)

    def put(name, arr):
        a, b = _off[name]
        c[:, a:b] = np.asarray(arr, np.float32).reshape(128, b - a)

    p = np.arange(128)
    put("ident", np.eye(128))
    put("U", (p[:, None] <= p[None, :]).astype(np.float32))
    put("ones", np.ones((128, 128)))
    put("cmask", np.where(p[None, :] >= p[:, None], 0.0, -30000.0))
    dm = np.zeros((128, 4, 128), np.float64)
    for h in range(4):
        rel = (p[None, :] - p[:, None]).astype(np.float64)
        dm[:, h, :] = np.where(rel >= 0, GAM[h] ** np.maximum(rel, 0.0), 0.0) / 16.0
    put("dmask", dm)
    put("qdec", np.stack([GAM[h] ** (p + 1.0) for h in range(4)], axis=1))
    put("kdec", np.stack([GAM[h] ** (127.0 - p) / 16.0 for h in range(4)], axis=1))
    invf = (1.0 / (np.float32(10000.0) ** np.linspace(0.0, 1.0, 128, dtype=np.float32))).astype(np.float32)
    put("invf", invf)
    put("convw", ssd_conv_w.reshape(4, 12, 128).transpose(2, 1, 0))
    put("convb", ssd_conv_b.reshape(12, 128).T)
    put("gng", ret_gn_gain.reshape(8, 128).T)
    put("gnb", ret_gn_bias.reshape(8, 128).T)
    put("ng", ssd_norm_gain.reshape(8, 128).T)
    put("dcol", np.repeat(ssd_d.reshape(16), 64).reshape(8, 128).T)
    put("dtb", np.broadcast_to(ssd_dt_bias.reshape(1, 16), (128, 16)))
    put("alog", np.broadcast_to(ssd_a_log.reshape(1, 16), (128, 16)))
    rank, seq = core % 4, core // 4
    oh = np.zeros(4)
    if rank > 0:
        oh[rank - 1] = 1.0
    put("ohalo", np.broadcast_to(oh, (128, 4)))
    mr = np.array([1.0 if (r8 // 4 == seq and r8 % 4 < rank) else 0.0 for r8 in range(7)])
    put("mr", np.broadcast_to(mr, (128, 7)))
    rw = np.array([[(GAM[h] ** 2048.0) if mr[r8] > 0 else 1.0 for h in range(4)] for r8 in range(7)]).reshape(28)
    put("retw", np.broadcast_to(rw, (128, 28)))
    return c


GU_W = 22 * 8192
D_W = 16 * 5632
IO_IN = 13 * 8192
IO_DT = IO_IN
IO_OUT = IO_IN + 256
IO_W = 69 * 2048
NST = 2048 + 1024 + 16
TWO_PI = 2.0 * math.pi
CW1 = 6.28125
CW2 = TWO_PI - CW1


def build(stage=3, tiles_a1=(3, 0, 1, 2)):
    nc = bass.Bass("TRN2", target_bir_lowering=False)
    S = Sched()
    es = contextlib.ExitStack()
    es.enter_context(nc.allow_low_precision("bf16 matmul operands, fp32 accumulation"))

    def dram_in(name, shape, dt):
        t = nc.dram_tensor(name, shape, dt, kind="ExternalInput")
        return Buf(t.ap(), name, 0, shape, 4 if dt != BF16 else 2, pdim=False)

    def dram_tmp(name, shape, dt):
        t = nc.dram_tensor(name, shape, dt)
        return Buf(t.ap(), name, 0, shape, 4 if dt != BF16 else 2, pdim=False)

    x_d = dram_in("x", [TOK, D], F32)
    pos_d = dram_in("pos", [128, TOK], I32)
    cst_d = dram_in("cst", [128, NCST], F32)
    lngb_d = dram_in("lngb", [6, 128, D], F32)
    wg_d = [dram_in("wg%d" % i, [16, 16, DFF], F32) for i in (1, 2)]
    wu_d = [dram_in("wu%d" % i, [16, 16, DFF], F32) for i in (1, 2)]
    wd_d = [dram_in("wd%d" % i, [NFC, 16, D], F32) for i in (1, 2)]
    win_d = dram_in("win", [16, 16, DIN], F32)
    wout_d = dram_in("wout", [16, 16, D], F32)
    out_t = nc.dram_tensor("out", [TOK, D], F32, kind="ExternalOutput")
    out_d = Buf(out_t.ap(), "out", 0, [TOK, D], 4, pdim=False)

    def ag_pair(name, width):
        j = width // 2048
        tl = nc.dram_tensor(name + "_l", [16 * j, 2048], BF16)
        tg = nc.dram_tensor(name + "_g", [128 * j, 2048], BF16)
        bl = Buf(tl.ap().rearrange("(pp j) n -> pp (j n)", pp=16), name + "_l", 0, [16, width], 2, pdim=False)
        bg = Buf(tg.ap().rearrange("(pp j) n -> pp (j n)", pp=128), name + "_g", 0, [128, width], 2, pdim=False)
        bl.raw, bg.raw = tl.ap(), tg.ap()
        return bl, bg

    gu_l, gu_g, d_l, d_g = [], [], [], []
    for i in range(2):
        a, b = ag_pair("gu%d" % i, GU_W)
        gu_l.append(a)
        gu_g.append(b)
        a, b = ag_pair("dd%d" % i, D_W)
        d_l.append(a)
        d_g.append(b)
    io_l, io_g = ag_pair("io", IO_W)
    GU_SPLIT = 6
    gu0a_l, gu0a_g = ag_pair("gu0a", GU_SPLIT * 8192)
    gu0b_l, gu0b_g = ag_pair("gu0b", (22 - GU_SPLIT) * 8192)

    def gu_step(i, s_):
        if i == 0:
            if s_ < GU_SPLIT:
                return gu0a_g[:, s_ * 8192:(s_ + 1) * 8192]
            return gu0b_g[:, (s_ - GU_SPLIT) * 8192:(s_ - GU_SPLIT + 1) * 8192]
        return gu_g[i][:, s_ * 8192:(s_ + 1) * 8192]
    x1s = dram_tmp("x1s", [NT, 128, 4, D], F32)
    x1Ts = dram_tmp("x1Ts", [NT, 128, 16, TT], BF16)
    hsend = dram_tmp("hsend", [128, 36], F32)
    hrecv = dram_tmp("hrecv", [4 * 128, 36], F32)
    ssend = dram_tmp("ssend", [128, NST], F32)
    srecv = dram_tmp("srecv", [8 * 128, NST], F32)

    cur = [SB_LO]

    def sb(name, shape, dt, addr=None):
        esz = 2 if dt == BF16 else 4
        n = esz
        for s in shape[1:]:
            n *= s
        if addr is None:
            addr = cur[0]
            cur[0] += (n + 63) // 64 * 64
            assert cur[0] <= SB_HI, (name, cur[0])
        else:
            assert addr + n <= SB_HI, (name, addr, n)
        h = nc.alloc_sbuf_tensor_at(name, shape, dt, offset=addr)
        return Buf(h, "sb", addr, shape, esz)

    cst = sb("cst", [128, NCST], F32)

    def C(name, a=None, b=None):
        lo, hi = _off[name]
        if a is None:
            return cst[:, lo:hi]
        return cst[:, lo + a:lo + (b if b is not None else a + 1)]

    ident_bf = sb("ident_bf", [128, 128], BF16)
    dmat = sb("dmat", [128, 8, 128], BF16)
    a_bc = sb("a_bc", [128, 16], F32)
    S_f = sb("S_f", [128, 2048], F32)
    S_b = sb("S_b", [128, 2048], BF16)
    st_f = sb("st_f", [128, 1024], F32)
    st_b = sb("st_b", [128, 1024], BF16)
    halo = sb("halo", [128, 12, 3], F32)
    halo0 = sb("halo0", [128, 12, 3], F32)
    xl3 = sb("xl3", [128, 16, 3], BF16)
    small = sb("small", [128, 384], F32)
    atot = sb("atot", [128, 16], F32)
    NW = 3
    Wp = [sb("W%d" % i, [128, 8192], BF16) for i in range(NW)]
    xT = [sb("xT%d" % i, [128, 16, TT], BF16) for i in range(2)]
    gb = [[sb("gb%d%d" % (i, k), [128, D], F32, addr=xT[i].base + k * 8192) for k in range(2)] for i in range(2)]
    xtok = sb("xtok", [128, 4, D], F32)
    rbuf = sb("rbuf", [128, NST], F32, addr=xtok.base)
    rbuf2 = sb("rbuf2", [128, NST], F32, addr=xtok.base + 16384)
    hr = sb("hr", [128, 4, 36], F32, addr=xtok.base + 16384)
    arena0 = cur[0]
    ARENA = SB_HI - arena0
    hT = sb("hT", [128, NFC, TT], BF16, addr=arena0)
    sgt = [sb("sgt%d" % i, [128, TT], F32, addr=arena0 + 45056 + i * 2048) for i in range(2)]
    x1T_stage = sb("x1Tst", [128, 16, TT], BF16, addr=arena0 + 28 * 1024)
    lnst = sb("lnst", [128, 24], F32, addr=arena0 + 45056 + 4096)
    lnmv = sb("lnmv", [128, 4, 4], F32, addr=arena0 + 45056 + 4096 + 128)
    assert 45056 + 4096 + 256 <= ARENA, ARENA
    gbB = [sb("gbB%d" % k, [128, D], F32, addr=arena0 + 51200 + k * 8192) for k in range(2)]

    mcur = [arena0]

    def ma(name, shape, dt):
        esz = 2 if dt == BF16 else 4
        n = esz
        for s in shape[1:]:
            n *= s
        addr = mcur[0]
        mcur[0] += (n + 63) // 64 * 64
        assert mcur[0] <= SB_HI, (name, mcur[0] - arena0, ARENA)
        return sb(name, shape, dt, addr=addr)

    mixT = ma("mixT", [128, 16, TT], BF16)
    mbase = mcur[0]
    xbcT = ma("xbcT", [128, 12, TT], BF16)
    sz = ma("sz", [128, 4, 1024], BF16)
    pre = [ma("pre%d" % i, [128, TT + 3], F32) for i in range(2)]
    acc = [ma("acc%d" % i, [128, TT], F32) for i in range(2)]
    dtt = ma("dtt", [128, 4, 16], F32)
    dAt = ma("dAt", [128, 4, 16], F32)
    Rm = ma("Rm", [128, 16, 128], F32)
    seg = sb("seg", [128, 16, 128], F32, addr=Rm.base)
    Pm = ma("Pm", [128, 16, 128], BF16)
    xdt = ma("xdt", [128, 16, 64], BF16)
    xsw = ma("xsw", [128, 16, 64], BF16)
    bmt = ma("bmt", [128, 256], BF16)
    ysb = ma("ysb", [128, 1024], F32)
    mcur[0] = mbase
    qT = ma("qT", [128, 8, TT], BF16)
    kT = ma("kT", [128, 8, TT], BF16)
    vt = ma("vt", [128, 4, 1024], BF16)
    sgT = ma("sgT", [128, 8, TT], BF16)
    ncs = [ma("ncs%d" % i, [128, TT], F32) for i in range(2)]
    rt = [ma("rt%d" % i, [128, TT], F32) for i in range(2)]
    kd = ma("kd", [128, 1024], BF16)
    Pt = ma("Pt", [128, 512], BF16)
    oa = ma("oa", [128, 1024], F32)
    ob = ma("ob", [128, 1024], F32)
    tmpT = sb("tmpT", [128, 8, 128], F32, addr=oa.base)
    posi = sb("posi", [128, TT], I32, addr=rt[1].base)

    psb = [nc.alloc_psum_tensor("ps%d" % b, [128, 512], F32) for b in range(8)]

    def PS(b, lo=0, hi=512, shape=None):
        ap = psb[b][:, lo:hi]
        if shape is not None:
            names = " ".join("a%d" % i for i in range(len(shape)))
            kw = {"a%d" % i: s for i, s in enumerate(shape)}
            ap = ap.rearrange("p (%s) -> p %s" % (names, names), **kw)
        return V(ap, "ps", b * 2048 + lo * 4, b * 2048 + hi * 4)

    def PSbf(b, lo=0, hi=1024, shape=None):
        ap = psb[b].bitcast(BF16)[:, lo:hi]
        if shape is not None:
            names = " ".join("a%d" % i for i in range(len(shape)))
            kw = {"a%d" % i: s for i, s in enumerate(shape)}
            ap = ap.rearrange("p (%s) -> p %s" % (names, names), **kw)
        return V(ap, "ps", b * 2048 + lo * 2, b * 2048 + hi * 2)

    pscur = [0]

    def psalloc(n=1):
        if pscur[0] % n:
            pscur[0] += n - pscur[0] % n
        if pscur[0] + n > 8:
            pscur[0] = 0
        b = pscur[0]
        pscur[0] += n
        return b

    def rs(v, shape):
        names = " ".join("a%d" % i for i in range(len(shape)))
        kw = {"a%d" % i: s for i, s in enumerate(shape)}
        return V(v.ap.rearrange("p (%s) -> p %s" % (names, names), **kw), v.space, v.lo, v.hi)

    def fl(v):
        nd = len(v.ap.shape) - 1
        names = " ".join("a%d" % i for i in range(nd))
        return V(v.ap.rearrange("p %s -> p (%s)" % (names, names)), v.space, v.lo, v.hi)

    def bc(v, axis, n):
        ap = v.ap.unsqueeze(axis)
        shp = list(ap.shape)
        shp[axis] = n
        return V(ap.broadcast_to(shp), v.space, v.lo, v.hi)

    def dma(eng, out, in_, slot, batch=False):
        S.op(eng, lambda e: e.dma_start(out=out.ap, in_=in_.ap), reads=[in_], writes=[out], slot=slot, batch=batch)

    def mm_group(out, pairs, reads, start=True, stop=True):
        def fn(e):
            n = len(pairs)
            ins = None
            for i, (l, r) in enumerate(pairs):
                ins = e.matmul(out.ap, lhsT=l, rhs=r, start=(start and i == 0), stop=(stop and i == n - 1))
            return ins
        S.op("pe", fn, reads=reads, writes=[out])

    def transp(out, in_, ident):
        S.op("pe", lambda e: e.transpose(out.ap, in_.ap, ident.ap), reads=[in_, ident], writes=[out])

    def act(out, in_, func, bias=None, scale=None):
        kw = {}
        rd = [in_]
        for nm, val in (("bias", bias), ("scale", scale)):
            if val is None:
                continue
            if isinstance(val, V):
                kw[nm] = val.ap
                rd.append(val)
            else:
                kw[nm] = val
        S.op("act", lambda e: e.activation(out=out.ap, in_=in_.ap, func=func, **kw), reads=rd, writes=[out])

    def tt(out, a, b, op, eng="dve"):
        S.op(eng, lambda e: e.tensor_tensor(out=out.ap, in0=a.ap, in1=b.ap, op=op), reads=[a, b], writes=[out])

    def ts(out, a, s1, s2, op0, op1=None, eng="dve"):
        rd = [a]
        v1 = s1.ap if isinstance(s1, V) else s1
        v2 = s2.ap if isinstance(s2, V) else s2
        if isinstance(s1, V):
            rd.append(s1)
        if isinstance(s2, V):
            rd.append(s2)
        if op1 is None:
            S.op(eng, lambda e: e.tensor_scalar(out=out.ap, in0=a.ap, scalar1=v1, scalar2=None, op0=op0),
                 reads=rd, writes=[out])
        else:
            S.op(eng, lambda e: e.tensor_scalar(out=out.ap, in0=a.ap, scalar1=v1, scalar2=v2, op0=op0, op1=op1),
                 reads=rd, writes=[out])

    def stt(out, a, s, b, op0, op1, eng="dve"):
        rd = [a, b]
        sv = s.ap if isinstance(s, V) else s
        if isinstance(s, V):
            rd.append(s)
        S.op(eng, lambda e: e.scalar_tensor_tensor(out=out.ap, in0=a.ap, scalar=sv, in1=b.ap, op0=op0, op1=op1),
             reads=rd, writes=[out])

    def copy(out, in_, eng="dve"):
        if eng == "act":
            act(out, in_, AF.Copy)
        else:
            S.op(eng, lambda e: e.tensor_copy(out=out.ap, in_=in_.ap), reads=[in_], writes=[out])

    def memset(v, val):
        S.op("dve", lambda e: e.memset(v.ap, val), reads=[], writes=[v])

    def bnstats(o, v):
        S.op("dve", lambda e: e.bn_stats(out=o.ap, in_=v.ap), reads=[v], writes=[o])

    def bnaggr(o, v):
        S.op("dve", lambda e: e.bn_aggr(out=o.ap, in_=v.ap), reads=[v], writes=[o])

    def rstd_from_var(var_v, tmp_v, out_v, eps):
        act(tmp_v, var_v, AF.Sqrt, bias=eps)
        S.op("dve", lambda e: e.reciprocal(out=out_v.ap, in_=tmp_v.ap), reads=[tmp_v], writes=[out_v])

    def allgather(in_buf, out_buf, groups, slot):
        iv, ov = in_buf.whole(), out_buf.whole()
        if hasattr(in_buf, "raw"):
            iv = V(in_buf.raw, iv.space, iv.lo, iv.hi)
            ov = V(out_buf.raw, ov.space, ov.lo, ov.hi)
        S.op("pool", lambda e: e.collective_compute("AllGather", ALU.bypass, replica_groups=groups,
                                                    ins=[iv.ap], outs=[ov.ap]),
             reads=[iv], writes=[ov], slot=slot, cc=True)

    ALL8 = [list(range(8))]
    SEQ4 = [[0, 1, 2, 3], [4, 5, 6, 7]]

    dma("sp", cst.whole(), cst_d.whole(), slot="cst")
    copy(ident_bf.whole(), C("ident"))
    for cc in range(8):
        ts(dmat[:, cc, :], C("ident"), C("dcol", cc), None, ALU.mult)
    act(a_bc.whole(), C("alog"), AF.Exp)
    ts(a_bc.whole(), a_bc.whole(), -1.0, None, ALU.mult)

    cast_n = {}

    def cast(dst_v, src_ap, slot):
        k = cast_n.get(dst_v.space, 0)
        cast_n[dst_v.space] = k + 1
        tok = V(dst_v.ap, dst_v.space, k, k + 1)
        S.op("pool", lambda e: e.dma_start(out=tok.ap, in_=src_ap), reads=[], writes=[tok], slot=slot, batch=True)

    def cast_ffn(i):
        parts = [(gu_l[i], gu_g[i], 0, 22)] if i else [(gu0a_l, gu0a_g, 0, GU_SPLIT), (gu0b_l, gu0b_g, GU_SPLIT, 22)]
        for pi_, (pl, pg, s0, s1) in enumerate(parts):
            lv = pl.h.rearrange("pp (s k dc f) -> pp s k dc f", s=s1 - s0, k=2, dc=16)
            for s_ in range(s0, s1):
                for k, src in ((0, wg_d[i]), (1, wu_d[i])):
                    sv = src.h.rearrange("dc pp (s f) -> pp s dc f", f=256)
                    cast(V(lv[:, s_ - s0, k], pl.space, 0, pl.nbytes), sv[:, s_], ("cgu", i, pi_))
            allgather(pl, pg, ALL8, ("ag_gu", i, pi_))
        lv = d_l[i].h.rearrange("pp (dr g fc n) -> pp dr g fc n", dr=4, g=4, fc=11)
        sv = wd_d[i].h.rearrange("(g fc) pp (dr n) -> pp dr g fc n", fc=11, n=512)
        for dr in range(4):
            for g in range(4):
                cast(V(lv[:, dr, g], d_l[i].space, 0, d_l[i].nbytes), sv[:, dr, g], ("cd", i))
        allgather(d_l[i], d_g[i], ALL8, ("ag_d", i))

    def cast_io():
        lv = io_l.h[:, 0:IO_IN].rearrange("pp (g dc n) -> pp g dc n", g=13, dc=16)
        sv = win_d.h[:, :, 0:6656].rearrange("dc pp (g n) -> pp g dc n", n=512)
        for g in range(13):
            cast(V(lv[:, g], io_l.space, 0, io_l.nbytes), sv[:, g], "cio")
        lv = io_l.h[:, IO_DT:IO_DT + 256].rearrange("pp (dc n) -> pp dc n", dc=16)
        sv = win_d.h[:, :, 6656:6672].rearrange("dc pp n -> pp dc n")
        cast(V(lv, io_l.space, 0, io_l.nbytes), sv, "cio")
        lv = io_l.h[:, IO_OUT:IO_OUT + 4 * 8192].rearrange("pp (dr mc n) -> pp dr mc n", dr=4, mc=16)
        sv = wout_d.h.rearrange("mc pp (dr n) -> pp dr mc n", n=512)
        for dr in range(4):
            cast(V(lv[:, dr], io_l.space, 0, io_l.nbytes), sv[:, dr], "cio")
        allgather(io_l, io_g, ALL8, "ag_io")

    cast_ffn(0)
    if stage >= 2:
        cast_io()
    if stage >= 3:
        cast_ffn(1)

    wslot = [0]

    def load_w(src_view, shape):
        i = wslot[0] % NW
        wslot[0] += 1
        nel = 1
        for s in shape[1:]:
            nel *= s
        dst = V(Wp[i].h[:, 0:nel], "sb", Wp[i].base, Wp[i].base + nel * 2)
        dma("sp", dst, src_view, slot=("W", i))
        view = Wp[i].h[:, 0:nel]
        if len(shape) > 2:
            names = " ".join("a%d" % k for k in range(len(shape) - 1))
            kw = {"a%d" % k: s for k, s in enumerate(shape[1:])}
            view = view.rearrange("p (%s) -> p %s" % (names, names), **kw)
        return view, dst

    def io_grp(g):
        return io_g[:, g * 8192:(g + 1) * 8192]

    def transpose_tile(src_tok, dstT):
        for dc in range(16):
            b = psalloc()
            for c in range(4):
                transp(PS(b, c * 128, (c + 1) * 128), src_tok[:, c, dc * 128:(dc + 1) * 128], C("ident"))
            copy(dstT[:, dc, :], PS(b), eng=("act" if dc % 2 == 0 else "dve"))

    def layer_norm(xt_buf, gbuf, ln_idx, eps):
        dma("sp", gbuf[0].whole(), lngb_d[2 * ln_idx], slot=("gb", 0))
        dma("sp", gbuf[1].whole(), lngb_d[2 * ln_idx + 1], slot=("gb", 1))
        for c in range(4):
            for q in range(4):
                bnstats(lnst[:, q * 6:(q + 1) * 6], xt_buf[:, c, q * 512:(q + 1) * 512])
            bnaggr(lnmv[:, c, 0:2], lnst.whole())
        rstd_from_var(lnmv[:, :, 1:2], lnmv[:, :, 2:3], lnmv[:, :, 3:4], eps)
        for c in range(4):
            ts(xt_buf[:, c, :], xt_buf[:, c, :], lnmv[:, c, 0:1], lnmv[:, c, 3:4], ALU.subtract, ALU.mult)
            tt(xt_buf[:, c, :], xt_buf[:, c, :], gbuf[0].whole(), ALU.mult)
            tt(xt_buf[:, c, :], xt_buf[:, c, :], gbuf[1].whole(), ALU.add)

    def ffn_tile(i, xTb, xt_buf, res_scale):
        for s in range(22):
            wv, wdst = load_w(gu_step(i, s), [128, 2, 16, 256])
            base = (s % 2) * 4
            for j in range(2):
                fc = 2 * s + j
                for k in range(2):
                    mm_group(PS(base + 2 * k + j),
                             [(wv[:, k, dc, j * 128:(j + 1) * 128], xTb[:, dc, :].ap) for dc in range(16)],
                             reads=[wdst, xTb.whole()])
                sg = sgt[fc % 2]
                act(sg.whole(), PS(base + j), AF.Silu)
                tt(hT[:, fc, :], sg.whole(), PS(base + 2 + j), ALU.mult)
        for dr in range(4):
            base = (dr % 2) * 4
            for g in range(4):
                o = (dr * 4 + g) * 5632
                wv, wdst = load_w(d_g[i][:, o:o + 5632], [128, 11, 512])
                for tcn in range(4):
                    mm_group(PS(base + tcn),
                             [(hT[:, g * 11 + f, tcn * 128:(tcn + 1) * 128].ap, wv[:, f, :]) for f in range(11)],
                             reads=[wdst, hT[:, g * 11:(g + 1) * 11, :]], start=(g == 0), stop=(g == 3))
            for tcn in range(4):
                v = xt_buf[:, tcn, dr * 512:(dr + 1) * 512]
                stt(v, PS(base + tcn), res_scale, v, ALU.mult, ALU.add)

    acum, tot, fs, te, et = (small[:, 0:16], small[:, 16:32], small[:, 32:48], small[:, 48:64], small[:, 64:80])
    gst = small[:, 96:120]
    gmv = sb("gmv", [128, 4, 4], F32, addr=small.base + 128 * 4)
    rst = small[:, 160:166]
    rmv = small[:, 168:176]
    tw = small[:, 176:192]
    hs = small[:, 192:228]

    def mixer_tile(t, xb, full):
        xw = xb.whole()
        wv, wd = load_w(io_g[:, IO_DT:IO_DT + 256], [128, 16, 16])
        b = psalloc()
        for c in range(4):
            mm_group(PS(b, c * 16, (c + 1) * 16),
                     [(xb[:, dc, c * 128:(c + 1) * 128].ap, wv[:, dc, :]) for dc in range(16)], reads=[wd, xw])
        tt(dtt.whole(), PS(b, 0, 64, shape=[4, 16]), bc(C("dtb"), 1, 4), ALU.add)
        act(dtt.whole(), dtt.whole(), AF.Exp)
        act(dtt.whole(), dtt.whole(), AF.Ln, bias=1.0)
        tt(dAt.whole(), dtt.whole(), bc(a_bc.whole(), 1, 4), ALU.mult)
        if full:
            for half in range(2):
                wv, wd = load_w(io_grp(8 + half), [128, 16, 512])
                for c in range(4):
                    b = psalloc()
                    mm_group(PS(b), [(xb[:, dc, c * 128:(c + 1) * 128].ap, wv[:, dc, :]) for dc in range(16)],
                             reads=[wd, xw])
                    act(sz[:, c, half * 512:(half + 1) * 512], PS(b), AF.Silu)
        for gi in range(3):
            wv, wd = load_w(io_grp(10 + gi), [128, 16, 512])
            for j in range(4):
                cc = gi * 4 + j
                b = psalloc()
                mm_group(PS(b), [(wv[:, dc, j * 128:(j + 1) * 128], xb[:, dc, :].ap) for dc in range(16)],
                         reads=[wd, xw])
                p, a = pre[cc % 2], acc[cc % 2]
                copy(p[:, 0:3], halo[:, cc, :], eng="act")
                copy(p[:, 3:TT + 3], PS(b), eng="act")
                ts(a.whole(), p[:, 0:TT], C("convw", cc * 4), None, ALU.mult)
                for k in range(1, 4):
                    stt(a.whole(), p[:, k:k + TT], C("convw", cc * 4 + k), a.whole(), ALU.mult, ALU.add)
                copy(halo[:, cc, :], p[:, TT:TT + 3], eng="act")
                act(xbcT[:, cc, :], a.whole(), AF.Silu, bias=C("convb", cc))
        for c in range(4):
            c0, c1_ = c * 128, (c + 1) * 128
            dAc = dAt[:, c, :]
            b = psalloc()
            mm_group(PS(b, 0, 16), [(C("U").ap, dAc.ap)], reads=[C("U"), dAc])
            mm_group(PS(b, 16, 32), [(C("ones").ap, dAc.ap)], reads=[C("ones"), dAc])
            copy(small[:, 0:32], PS(b, 0, 32), eng="act")
            act(fs, acum, AF.Exp)
            tt(te, tot, acum, ALU.subtract)
            act(te, te, AF.Exp)
            act(et, tot, AF.Exp)
            if not full:
                tt(atot.whole(), atot.whole(), tot, ALU.add)
            b = psalloc()
            for cc in range(8):
                transp(PSbf(b, cc * 128, (cc + 1) * 128), xbcT[:, cc, c0:c1_], ident_bf.whole())
            tt(xdt.whole(), PSbf(b, shape=[16, 64]), bc(dtt[:, c, :], 2, 64), ALU.mult)
            tt(xsw.whole(), xdt.whole(), bc(te, 2, 64), ALU.mult)
            b = psalloc()
            for g in range(2):
                transp(PSbf(b, g * 128, (g + 1) * 128), xbcT[:, 8 + g, c0:c1_], ident_bf.whole())
            copy(bmt.whole(), PSbf(b, 0, 256), eng="act")
            if full:
                tt(Rm.whole(), bc(C("U"), 1, 16), bc(dAc, 2, 128), ALU.mult)
                b4 = psalloc(4)
                for q in range(4):
                    rq = fl(Rm[:, q * 4:(q + 1) * 4, :])
                    mm_group(PS(b4 + q), [(C("ones").ap, rq.ap)], reads=[C("ones"), rq])
                for q in range(4):
                    tt(seg[:, q * 4:(q + 1) * 4, :], PS(b4 + q, shape=[4, 128]),
                       bc(small[:, q * 4:(q + 1) * 4], 2, 128), ALU.subtract)
                tt(seg.whole(), seg.whole(), bc(C("cmask"), 1, 16), ALU.add)
                act(seg.whole(), seg.whole(), AF.Exp)
                b = psalloc()
                for g in range(2):
                    bm, cm = xbcT[:, 8 + g, c0:c1_], xbcT[:, 10 + g, c0:c1_]
                    mm_group(PS(b, g * 128, (g + 1) * 128), [(bm.ap, cm.ap)], reads=[bm, cm])
                for g in range(2):
                    tt(Pm[:, g * 8:(g + 1) * 8, :], seg[:, g * 8:(g + 1) * 8, :],
                       bc(PS(b, g * 128, (g + 1) * 128), 1, 8), ALU.mult)
                by = psalloc(2)
                for h in range(16):
                    cc, half = h // 2, h % 2
                    col = (h % 8) * 64
                    xc = xbcT[:, cc, c0:c1_]
                    dm = dmat[:, cc, half * 64:(half + 1) * 64]
                    mm_group(PS(by + h // 8, col, col + 64), [(xc.ap, dm.ap), (Pm[:, h, :].ap, xdt[:, h, :].ap)],
                             reads=[xc, dm, Pm[:, h, :], xdt[:, h, :]])
                bi = psalloc(2)
                for g in range(2):
                    cm, sg_ = xbcT[:, 10 + g, c0:c1_], st_b[:, g * 512:(g + 1) * 512]
                    mm_group(PS(bi + g), [(cm.ap, sg_.ap)], reads=[cm, sg_])
                for g in range(2):
                    yv = ysb[:, g * 512:(g + 1) * 512]
                    tt(rs(yv, [8, 64]), PS(bi + g, shape=[8, 64]), bc(small[:, 32 + g * 8:32 + (g + 1) * 8], 2, 64),
                       ALU.mult)
                    tt(yv, yv, PS(by + g), ALU.add)
                tt(ysb.whole(), ysb.whole(), sz[:, c, :], ALU.mult)
                for g in range(2):
                    yv = ysb[:, g * 512:(g + 1) * 512]
                    bnstats(rst, yv)
                    bnaggr(rmv[:, 0:2] if False else small[:, 168:170], rst)
                    stt(small[:, 170:171], small[:, 168:169], small[:, 168:169], small[:, 169:170], ALU.mult, ALU.add)
                    rstd_from_var(small[:, 170:171], small[:, 171:172], small[:, 172 + g:173 + g], EPS)
                    ts(yv, yv, small[:, 172 + g:173 + g], None, ALU.mult)
                for half in range(2):
                    b = psalloc()
                    for j in range(4):
                        cc = half * 4 + j
                        transp(PS(b, j * 128, (j + 1) * 128), ysb[:, cc * 128:(cc + 1) * 128], C("ident"))
                    for j in range(4):
                        cc = half * 4 + j
                        act(mixT[:, 8 + cc, c0:c1_], PS(b, j * 128, (j + 1) * 128), AF.Identity, scale=C("ng", cc))
            bs = psalloc(2)
            for g in range(2):
                bmv, xv = bmt[:, g * 128:(g + 1) * 128], fl(xsw[:, g * 8:(g + 1) * 8, :])
                mm_group(PS(bs + g), [(bmv.ap, xv.ap)], reads=[bmv, xv])
            for g in range(2):
                sv = st_f[:, g * 512:(g + 1) * 512]
                tt(rs(sv, [8, 64]), rs(sv, [8, 64]), bc(small[:, 64 + g * 8:64 + (g + 1) * 8], 2, 64), ALU.mult)
                tt(sv, sv, PS(bs + g), ALU.add)
            copy(st_b.whole(), st_f.whole(), eng="act")

        dma("sp", posi.whole(), pos_d[:, t * TT:(t + 1) * TT], slot="posi")
        th, kf = rt[0].whole(), ncs[0].whole()
        copy(th, posi.whole())
        ts(th, th, C("invf"), None, ALU.mult)
        ts(kf, th, 1.0 / TWO_PI, None, ALU.mult)
        copy(posi.whole(), kf)
        copy(kf, posi.whole())
        stt(th, kf, -CW1, th, ALU.mult, ALU.add)
        stt(th, kf, -CW2, th, ALU.mult, ALU.add)
        ts(th, th, 3.141592, -3.141592, ALU.min, ALU.max)
        sinv, cosv = ncs[1].whole(), ncs[0].whole()
        act(sinv, th, AF.Sin)
        ts(rt[1].whole(), th, -1.0, None, ALU.mult)
        tt(th, th, rt[1].whole(), ALU.max)
        act(cosv, th, AF.Sin, scale=-1.0, bias=math.pi / 2.0)
        for (isq, dst, grp0) in ((True, qT, 0), (False, kT, 2)):
            if isq and not full:
                continue
            for gi in range(2):
                wv, wd = load_w(io_grp(grp0 + gi), [128, 16, 512])
                bb = psalloc(4)
                for j in range(4):
                    mm_group(PS(bb + j), [(wv[:, dc, j * 128:(j + 1) * 128], xb[:, dc, :].ap) for dc in range(16)],
                             reads=[wd, xw])
                for hh in range(2):
                    pe_, po_ = PS(bb + 2 * hh), PS(bb + 2 * hh + 1)
                    ce = gi * 4 + 2 * hh
                    r0, r1 = rt[0].whole(), rt[1].whole()
                    tt(r0, pe_, cosv, ALU.mult)
                    tt(r1, po_, sinv, ALU.mult)
                    tt(dst[:, ce, :], r0, r1, ALU.subtract)
                    tt(r0, po_, cosv, ALU.mult)
                    tt(r1, pe_, sinv, ALU.mult)
                    tt(dst[:, ce + 1, :], r0, r1, ALU.add)
        for half in range(2):
            wv, wd = load_w(io_grp(4 + half), [128, 16, 512])
            for c in range(4):
                b = psalloc()
                mm_group(PS(b), [(xb[:, dc, c * 128:(c + 1) * 128].ap, wv[:, dc, :]) for dc in range(16)],
                         reads=[wd, xw])
                copy(vt[:, c, half * 512:(half + 1) * 512], PS(b), eng="act")
        if full:
            for gi in range(2):
                wv, wd = load_w(io_grp(6 + gi), [128, 16, 512])
                for j in range(4):
                    b = psalloc()
                    mm_group(PS(b), [(wv[:, dc, j * 128:(j + 1) * 128], xb[:, dc, :].ap) for dc in range(16)],
                             reads=[wd, xw])
                    act(sgT[:, gi * 4 + j, :], PS(b), AF.Silu)
        for c in range(4):
            c0, c1_ = c * 128, (c + 1) * 128
            b = psalloc()
            for kc in range(8):
                transp(PSbf(b, kc * 128, (kc + 1) * 128), kT[:, kc, c0:c1_], ident_bf.whole())
            for h in range(4):
                ts(kd[:, h * 256:(h + 1) * 256], PSbf(b, h * 256, (h + 1) * 256), C("kdec", h), None, ALU.mult)
            if full:
                b = psalloc()
                for h in range(4):
                    kk, qq = kT[:, 2 * h:2 * h + 2, c0:c1_], qT[:, 2 * h:2 * h + 2, c0:c1_]
                    mm_group(PS(b, h * 128, (h + 1) * 128),
                             [(kT[:, 2 * h + e, c0:c1_].ap, qT[:, 2 * h + e, c0:c1_].ap) for e in range(2)],
                             reads=[kk, qq])
                tt(Pt.whole(), PS(b), C("dmask"), ALU.mult)
                ba = psalloc(2)
                bo = psalloc(2)
                for h in range(4):
                    col = (h % 2) * 256
                    pv, vv = Pt[:, h * 128:(h + 1) * 128], vt[:, c, h * 256:(h + 1) * 256]
                    mm_group(PS(ba + h // 2, col, col + 256), [(pv.ap, vv.ap)], reads=[pv, vv])
                    qq, ss = qT[:, 2 * h:2 * h + 2, c0:c1_], S_b[:, 2 * h * 256:(2 * h + 2) * 256]
                    mm_group(PS(bo + h // 2, col, col + 256),
                             [(qT[:, 2 * h + e, c0:c1_].ap, S_b[:, (2 * h + e) * 256:(2 * h + e + 1) * 256].ap)
                              for e in range(2)], reads=[qq, ss])
                for g2 in range(2):
                    copy(oa[:, g2 * 512:(g2 + 1) * 512], PS(ba + g2), eng="act")
                for h in range(4):
                    col = (h % 2) * 256
                    stt(ob[:, h * 256:(h + 1) * 256], PS(bo + h // 2, col, col + 256), C("qdec", h),
                        oa[:, h * 256:(h + 1) * 256], ALU.mult, ALU.add)
                for h in range(4):
                    bnstats(small[:, 96 + h * 6:102 + h * 6], ob[:, h * 256:(h + 1) * 256])
                    bnaggr(gmv[:, h, 0:2], small[:, 96 + h * 6:102 + h * 6])
                rstd_from_var(gmv[:, :, 1:2], gmv[:, :, 2:3], gmv[:, :, 3:4], EPS)
                for h in range(4):
                    ov = ob[:, h * 256:(h + 1) * 256]
                    ts(ov, ov, gmv[:, h, 0:1], gmv[:, h, 3:4], ALU.subtract, ALU.mult)
                for half in range(2):
                    b = psalloc()
                    for j in range(4):
                        mc = half * 4 + j
                        transp(PS(b, j * 128, (j + 1) * 128), ob[:, mc * 128:(mc + 1) * 128], C("ident"))
                    for j in range(4):
                        mc = half * 4 + j
                        act(tmpT[:, mc, :], PS(b, j * 128, (j + 1) * 128), AF.Identity, scale=C("gng", mc),
                            bias=C("gnb", mc))
                tt(mixT[:, 0:8, c0:c1_], tmpT.whole(), sgT[:, :, c0:c1_], ALU.mult)
            bs = psalloc(4)
            for h in range(4):
                for e in range(2):
                    idx = 2 * h + e
                    col = (idx % 2) * 256
                    kv, vv = kd[:, idx * 128:(idx + 1) * 128], vt[:, c, h * 256:(h + 1) * 256]
                    mm_group(PS(bs + idx // 2, col, col + 256), [(kv.ap, vv.ap)], reads=[kv, vv])
            for h in range(4):
                sv = S_f[:, h * 512:(h + 1) * 512]
                stt(sv, sv, GAM[h] ** 128.0, PS(bs + h), ALU.mult, ALU.add)
            copy(S_b.whole(), S_f.whole(), eng="act")

    c1 = 0.5 / ALPHA
    eps1 = EPS / (ALPHA * ALPHA)
    for n, t in enumerate(tiles_a1):
        xb = xT[n % 2]
        dma("sp", xtok.whole(), V(x_d.h[t * TT:(t + 1) * TT, :].rearrange("(c p) d -> p c d", p=128), "x", t, t + 1),
            slot="xtok")
        transpose_tile(xtok, xb)
        if DBG != 1:
            ffn_tile(0, xb, xtok, c1)
        if DBG not in (1, 2):
            layer_norm(xtok, gb[n % 2], 0, eps1)
        if stage == 1:
            dma("sp", V(out_d.h[t * TT:(t + 1) * TT, :].rearrange("(c p) d -> p c d", p=128), "out", t, t + 1),
                xtok.whole(), slot="xtok_st")
            continue
        dma("sp", x1s[t], xtok.whole(), slot="xtok_st")
        transpose_tile(xtok, x1T_stage)
        dma("sp", x1Ts[t], x1T_stage.whole(), slot="x1T_st")
        if t == 3:
            copy(xl3.whole(), x1T_stage[:, :, TT - 3:TT])
        if n == 1 and DBG != 5:
            b = psalloc()
            for gi in range(3):
                wv, wd = load_w(io_grp(10 + gi), [128, 16, 512])
                for j in range(4):
                    cc = gi * 4 + j
                    mm_group(PS(b, cc * 3, cc * 3 + 3),
                             [(wv[:, dc, j * 128:(j + 1) * 128], xl3[:, dc, :].ap) for dc in range(16)],
                             reads=[wd, xl3.whole()])
            copy(hs, PS(b, 0, 36))
            dma("sp", hsend.whole(), hs, slot="hs")
            allgather(hsend, hrecv, SEQ4, "ag_h")

    if stage >= 2 and DBG != 5:
        dma("sp", hr.whole(), V(hrecv.h.rearrange("(r p) n -> p r n", p=128), hrecv.space, 0, hrecv.nbytes), slot="hr")
        h0 = fl(halo0.whole())
        ts(h0, hr[:, 0, :], C("ohalo", 0), None, ALU.mult)
        for r in range(1, 4):
            stt(h0, hr[:, r, :], C("ohalo", r), h0, ALU.mult, ALU.add)
        for v in (S_f, S_b, st_f, st_b, atot):
            memset(v.whole(), 0.0)
        copy(halo.whole(), halo0.whole())
        for t in range(NT if DBG != 3 else 0):
            xb = xT[t % 2]
            dma("sp", xb.whole(), x1Ts[t], slot=("xT", t % 2))
            mixer_tile(t, xb, False)
    if stage >= 2 and DBG not in (3, 4, 5):
        dma("sp", ssend[:, 0:2048], S_f.whole(), slot="ss0")
        dma("sp", ssend[:, 2048:3072], st_f.whole(), slot="ss1")
        dma("sp", ssend[:, 3072:3088], atot.whole(), slot="ss2")
        allgather(ssend, srecv, ALL8, "ag_s")
        for v in (S_f, st_f):
            memset(v.whole(), 0.0)
        for i_, r in enumerate((0, 1, 2, 4, 5, 6)):
            rb_ = (rbuf, rbuf2)[i_ % 2]
            dma("sp", rb_.whole(), srecv[r * 128:(r + 1) * 128, :], slot=("rbuf", i_ % 2))
            ts(rb_[:, 0:3072], rb_[:, 0:3072], C("mr", r), None, ALU.mult)
            for h in range(4):
                sv = S_f[:, h * 512:(h + 1) * 512]
                stt(sv, sv, C("retw", r * 4 + h), rb_[:, h * 512:(h + 1) * 512], ALU.mult, ALU.add)
            ts(tw, rb_[:, 3072:3088], C("mr", r), None, ALU.mult)
            act(tw, tw, AF.Exp)
            for g in range(2):
                sv = st_f[:, g * 512:(g + 1) * 512]
                tt(rs(sv, [8, 64]), rs(sv, [8, 64]), bc(small[:, 176 + g * 8:176 + (g + 1) * 8], 2, 64), ALU.mult)
                tt(sv, sv, rb_[:, 2048 + g * 512:2048 + (g + 1) * 512], ALU.add)
        copy(S_b.whole(), S_f.whole(), eng="act")
        copy(st_b.whole(), st_f.whole(), eng="act")
        copy(halo.whole(), halo0.whole())

    if stage >= 3:
        def store_out(t_):
            dma("sp", V(out_d.h[t_ * TT:(t_ + 1) * TT, :].rearrange("(c p) d -> p c d", p=128), "out", t_, t_ + 1),
                xtok.whole(), slot="xtok_st")

        for t in range(NT):
            xb, xo = xT[0], xT[1]
            if t == 0:
                dma("sp", xb.whole(), x1Ts[t], slot=("xT", 0))
            mixer_tile(t, xb, True)
            if t > 0:
                store_out(t - 1)
            if t + 1 < NT:
                dma("sp", xb.whole(), x1Ts[t + 1], slot=("xT", 0))
            dma("sp", xtok.whole(), x1s[t], slot="xtok")
            for dr in range(4):
                wv, wd = load_w(io_g[:, IO_OUT + dr * 8192:IO_OUT + (dr + 1) * 8192], [128, 16, 512])
                for tcn in range(4):
                    b = psalloc()
                    mm_group(PS(b), [(mixT[:, mc, tcn * 128:(tcn + 1) * 128].ap, wv[:, mc, :]) for mc in range(16)],
                             reads=[wd, mixT.whole()])
                    v = xtok[:, tcn, dr * 512:(dr + 1) * 512]
                    stt(v, PS(b), 1.0 / ALPHA, v, ALU.mult, ALU.add)
            layer_norm(xtok, gb[1], 1, eps1)
            transpose_tile(xtok, xo)
            ffn_tile(1, xo, xtok, c1)
            layer_norm(xtok, gbB, 2, eps1)
            if t == NT - 1:
                store_out(t)
    elif stage == 2:
        dma("sp", V(out_d.h[0:128, :], "out", 0, 1), S_f.whole(), slot="dbg0")
        dma("sp", V(out_d.h[128:256, 0:1024], "out", 1, 2), st_f.whole(), slot="dbg1")

    S.emit(nc)
    es.close()
    return nc


def _perm_cols():
    idx = np.arange(DIN)
    for base in (0, 1024):
        for h in range(4):
            o = base + h * 256
            idx[o:o + 128] = o + np.arange(0, 256, 2)
            idx[o + 128:o + 256] = o + np.arange(1, 256, 2)
    return idx


def make_in_maps(x, positions, ffn1_w_gate, ffn1_w_up, ffn1_w_down, ln1_gain, ln1_bias, mix_w_in, ret_gn_gain,
                 ret_gn_bias, ssd_conv_w, ssd_conv_b, ssd_dt_bias, ssd_a_log, ssd_d, ssd_norm_gain, mix_w_out,
                 ln2_gain, ln2_bias, ffn2_w_gate, ffn2_w_up, ffn2_w_down, ln3_gain, ln3_bias):
    A = lambda a: np.asarray(a, np.float32)
    win = A(mix_w_in)[0][:, _perm_cols()]
    lngb = np.ascontiguousarray(np.stack([np.broadcast_to(A(v).reshape(1, D), (128, D)) for v in
                                          (ln1_gain, ln1_bias, ln2_gain, ln2_bias, ln3_gain, ln3_bias)]))
    full = {"wg1": A(ffn1_w_gate)[0], "wu1": A(ffn1_w_up)[0], "wd1": A(ffn1_w_down)[0],
            "wg2": A(ffn2_w_gate)[0], "wu2": A(ffn2_w_up)[0], "wd2": A(ffn2_w_down)[0],
            "win": win, "wout": A(mix_w_out)[0]}
    xs = A(x)
    ps = np.asarray(positions, np.int32)
    maps = []
    for c in range(8):
        b, r = c // 4, c % 4
        m = {"lngb": lngb}
        for k, w in full.items():
            rows, cols = w.shape
            m[k] = np.ascontiguousarray(w.reshape(rows // 128, 128, cols)[:, 16 * c:16 * (c + 1), :])
        m["x"] = np.ascontiguousarray(xs[b, r * TOK:(r + 1) * TOK, :])
        m["pos"] = np.ascontiguousarray(np.broadcast_to(ps[b, r * TOK:(r + 1) * TOK][None, :], (128, TOK)))
        m["cst"] = _host_consts(c, A(ssd_conv_w)[0], A(ssd_conv_b)[0], A(ret_gn_gain)[0], A(ret_gn_bias)[0],
                                A(ssd_norm_gain)[0], A(ssd_d)[0], A(ssd_dt_bias)[0], A(ssd_a_log)[0])
        maps.append(m)
    return maps


_NC_CACHE = {}


def kernel(**inputs):
    stage = 3
    if stage not in _NC_CACHE:
        _NC_CACHE[stage] = build(stage)
    nc = _NC_CACHE[stage]
    maps = make_in_maps(**inputs)
    res = run_bass_kernel_spmd(nc, maps, core_ids=list(range(8)), trace=True)
    out = np.empty((2, 8192, D), np.float32)
    for c in range(8):
        b, r = c // 4, c % 4
        out[b, r * TOK:(r + 1) * TOK, :] = np.asarray(res.results[c]["out"], np.float32)
    return out
```
